# Optimizing a Trainium2 kernel written in Bass

```python
import jax, jax.numpy as jnp
from jax import lax
import numpy as np

D_MODEL = 1024
BATCH = 32
SEQ = 2048
DEPTH = 2
DEC_BATCH = 1
DEC_SEQ = 16384
PAST_LEN = 128

NORM_EPS = 1e-6
D_FF = 2816
N_EVEN = (DEPTH + 1) // 2
N_ODD = DEPTH // 2
D_A = D_MODEL // 2
SGU_GROUPS = 4
SGU_CHUNK = 128
SGU_GW = D_A // SGU_GROUPS
D_B = D_MODEL // 2
RWKV_HEAD = 64
RWKV_HEADS = D_B // RWKV_HEAD
RWKV_LORA_W = 64
RWKV_LORA_A = 64
RWKV_LORA = RWKV_LORA_W + RWKV_LORA_A
RWKV_GN_EPS = 64e-5
D_AB_IN = 2 * D_A + 4 * D_B + 2 * RWKV_LORA
ATTN_HEAD_DIM = 64
ATTN_HEADS = D_MODEL // ATTN_HEAD_DIM
D_ATTN = ATTN_HEADS * ATTN_HEAD_DIM
DILATED_BRANCHES = ((128, 1), (512, 4), (2048, 16))
NEG_INF = -1e30

kernel_name = 'hybrid_sgu_rwkv7_dilated_encoder'


def rmsnorm(x, g, eps=NORM_EPS):
    xf = x.astype(jnp.float32)
    y = xf * lax.rsqrt(jnp.mean(xf * xf, axis=-1, keepdims=True) + eps)
    return (y * g.astype(jnp.float32)).astype(x.dtype)


def swiglu_ffn(x, w_in, w_out):
    gate, up = jnp.split(x @ w_in, 2, axis=-1)
    return (jax.nn.silu(gate) * up) @ w_out


def shift_prev(z):
    return jnp.pad(z, ((0, 0), (1, 0), (0, 0)))[:, :-1]


def shift_next(z):
    return jnp.pad(z, ((0, 0), (0, 1), (0, 0)))[:, 1:]


def spatial_gating(u, v, norm_g, w_s, b_s):
    B, T, _ = u.shape
    u = jax.nn.gelu(u)
    v = rmsnorm(jax.nn.gelu(v), norm_g)
    vc = v.reshape(B, T // SGU_CHUNK, SGU_CHUNK, SGU_GROUPS, SGU_GW)
    mixed = jnp.einsum('gqp,bnpgc->bnqgc', w_s, vc) + b_s.T[None, None, :, :, None]
    return u * mixed.reshape(B, T, D_A)


def rwkv7_scan(r, w, k, v, kk, kka, reverse):
    B, T, H, N = r.shape

    def step(S, inp):
        r_t, w_t, k_t, v_t, kk_t, kka_t = inp
        sa = jnp.einsum('bhvk,bhk->bhv', S, kk_t)
        S = (S * w_t[:, :, None, :] - sa[..., None] * kka_t[:, :, None, :]
             + v_t[..., None] * k_t[:, :, None, :])
        return S, jnp.einsum('bhvk,bhk->bhv', S, r_t)

    S0 = jnp.zeros((B, H, N, N), jnp.float32)
    xs = tuple(jnp.swapaxes(t, 0, 1) for t in (r, w, k, v, kk, kka))
    _, y = lax.scan(step, S0, xs, reverse=reverse)
    return jnp.swapaxes(y, 0, 1)


def rwkv7_bidirectional(rkv, wa, g, mu_rkv, mu_wa, w0, w_up, a0, a_up, k_k, k_a, r_k, gn_w, gn_b):
    B, T, _ = rkv.shape
    rkv = rkv.astype(jnp.float32)
    wa = wa.astype(jnp.float32)
    heads = lambda t: t.reshape(B, T, RWKV_HEADS, RWKV_HEAD)
    ys, bonuses = [], []
    for dirn, (shift, rev) in enumerate(((shift_prev, False), (shift_next, True))):
        z = rkv + (shift(rkv) - rkv) * mu_rkv[dirn]
        zwa = wa[:, :, dirn]
        zwa = zwa + (shift(zwa) - zwa) * mu_wa[dirn]
        r, k, v = jnp.split(z, 3, axis=-1)
        wd, ad = jnp.split(zwa, [RWKV_LORA_W], axis=-1)
        w_raw = w0[dirn] + jnp.tanh(wd) @ w_up[dirn]
        decay = jnp.exp(-jnp.exp(-jax.nn.softplus(-w_raw) - 0.5))
        a = jax.nn.sigmoid(a0[dirn] + ad @ a_up[dirn])
        kk = heads(k * k_k)
        kk = kk / jnp.maximum(jnp.sqrt(jnp.sum(kk * kk, axis=-1, keepdims=True)), 1e-12)
        k = k * (1.0 + (a - 1.0) * k_a)
        rh, kh, vh = heads(r), heads(k), heads(v)
        ys.append(rwkv7_scan(rh, heads(decay), kh, vh, kk, kk * heads(a), rev))
        bonuses.append(jnp.sum(rh * kh * r_k, axis=-1, keepdims=True) * vh)
    y = ys[0] + ys[1]
    mean = jnp.mean(y, axis=-1, keepdims=True)
    var = jnp.mean(jnp.square(y - mean), axis=-1, keepdims=True)
    y = ((y - mean) * lax.rsqrt(var + RWKV_GN_EPS)).reshape(B, T, D_B) * gn_w + gn_b
    y = (y + (bonuses[0] + bonuses[1]).reshape(B, T, D_B)) * jax.nn.sigmoid(g.astype(jnp.float32))
    return y


def sgu_rwkv_mixer(h, w_in, w_out, sgu_norm, sgu_w, sgu_b, mu_rkv, mu_wa, w0, w_up, a0, a_up,
                   k_k, k_a, r_k, gn_w, gn_b):
    B, T, _ = h.shape
    p = h @ w_in
    u, v, rkv, g, wa = jnp.split(p, [D_A, 2 * D_A, 2 * D_A + 3 * D_B, 2 * D_A + 4 * D_B], axis=-1)
    y_a = spatial_gating(u, v, sgu_norm, sgu_w, sgu_b)
    y_b = rwkv7_bidirectional(rkv, wa.reshape(B, T, 2, RWKV_LORA), g, mu_rkv, mu_wa, w0, w_up,
                              a0, a_up, k_k, k_a, r_k, gn_w, gn_b)
    return jnp.concatenate([y_a, y_b.astype(y_a.dtype)], axis=-1) @ w_out


def alibi_slopes():
    return 2.0 ** (-8.0 * jnp.arange(1, ATTN_HEADS + 1, dtype=jnp.float32) / ATTN_HEADS)


def dilated_branch(q, k, v, slopes, window, dilation):
    B, T, H, E = q.shape
    rad = window // (2 * dilation)
    L = T // dilation
    nb = -(-L // rad)
    Lp = nb * rad

    def to_sub(t):
        return jnp.swapaxes(t.reshape(B, L, dilation, H, E), 1, 2)

    def windows(t):
        tp = jnp.pad(t, ((0, 0), (0, 0), (rad, Lp - L + rad), (0, 0), (0, 0)))
        tp = tp.reshape(B, dilation, nb + 2, rad, H, E)
        return jnp.concatenate([tp[:, :, 0:nb], tp[:, :, 1:nb + 1], tp[:, :, 2:nb + 2]], axis=3)

    qb = jnp.pad(to_sub(q), ((0, 0), (0, 0), (0, Lp - L), (0, 0), (0, 0))).reshape(B, dilation, nb, rad, H, E)
    kw, vw = windows(to_sub(k)), windows(to_sub(v))
    qi = jnp.arange(Lp).reshape(nb, rad)
    kj = jnp.arange(nb)[:, None] * rad - rad + jnp.arange(3 * rad)[None, :]
    dist = jnp.abs(qi[:, :, None] - kj[:, None, :])
    valid = (dist <= rad) & (kj[:, None, :] >= 0) & (kj[:, None, :] < L)
    bias = jnp.where(valid, -slopes[:, None, None, None] * (dist * dilation).astype(jnp.float32),
                     NEG_INF)
    s = jnp.einsum('bdnqhe,bdnkhe->bdhnqk', qb, kw) + bias
    m = jnp.max(s, axis=-1, keepdims=True)
    p = jnp.exp(s - m)
    den = jnp.sum(p, axis=-1)
    o = jnp.einsum('bdhnqk,bdnkhe->bdnqhe', p, vw) / jnp.moveaxis(den, 2, -1)[..., None]
    lse = jnp.moveaxis(m[..., 0] + jnp.log(den), 2, -1)
    o = jnp.swapaxes(o.reshape(B, dilation, Lp, H, E)[:, :, :L], 1, 2).reshape(B, T, H, E)
    lse = jnp.swapaxes(lse.reshape(B, dilation, Lp, H)[:, :, :L], 1, 2).reshape(B, T, H)
    return o, lse


def dilated_attention(h, w_in, w_out, q_norm, k_norm):
    B, T, _ = h.shape
    qkv = (h @ w_in).astype(jnp.float32).reshape(B, T, 3, ATTN_HEADS, ATTN_HEAD_DIM)
    q = rmsnorm(qkv[:, :, 0], q_norm) * (ATTN_HEAD_DIM ** -0.5)
    k = rmsnorm(qkv[:, :, 1], k_norm)
    v = qkv[:, :, 2]
    slopes = alibi_slopes()
    outs, lses = [], []
    for window, dilation in DILATED_BRANCHES:
        o_b, lse_b = dilated_branch(q, k, v, slopes, window, dilation)
        outs.append(o_b)
        lses.append(lse_b)
    wts = jax.nn.softmax(jnp.stack(lses), axis=0)
    o = jnp.sum(wts[..., None] * jnp.stack(outs), axis=0)
    return o.reshape(B, T, D_ATTN).astype(h.dtype) @ w_out


def setup_inputs(seed: int = 0) -> dict:
    key = jax.random.key(seed)
    ks = iter(jax.random.split(key, 40))
    nrm = lambda shape, scale: jax.random.normal(next(ks), shape, jnp.float32) * scale
    unif = lambda shape, lo, hi: jax.random.uniform(next(ks), shape, jnp.float32, lo, hi)
    D, F = D_MODEL, D_FF
    return {
        'x_prompt': nrm((BATCH, SEQ, D), 1.0),
        'x_sample': nrm((DEC_BATCH, DEC_SEQ, D), 1.0),
        'ffn1_norm': 1.0 + nrm((DEPTH, D), 0.02),
        'ffn1_w_in': nrm((DEPTH, D, 2 * F), D ** -0.5),
        'ffn1_w_out': nrm((DEPTH, F, D), F ** -0.5),
        'mix_norm': 1.0 + nrm((DEPTH, D), 0.02),
        'ffn2_norm': 1.0 + nrm((DEPTH, D), 0.02),
        'ffn2_w_in': nrm((DEPTH, D, 2 * F), D ** -0.5),
        'ffn2_w_out': nrm((DEPTH, F, D), F ** -0.5),
        'block_norm': 1.0 + nrm((DEPTH, D), 0.02),
        'ab_w_in': nrm((N_EVEN, D, D_AB_IN), D ** -0.5),
        'ab_w_out': nrm((N_EVEN, D_A + D_B, D), (D_A + D_B) ** -0.5),
        'sgu_norm': 1.0 + nrm((N_EVEN, D_A), 0.02),
        'sgu_w': nrm((N_EVEN, SGU_GROUPS, SGU_CHUNK, SGU_CHUNK), SGU_CHUNK ** -0.5),
        'sgu_b': 1.0 + nrm((N_EVEN, SGU_GROUPS, SGU_CHUNK), 0.02),
        'rwkv_mu_rkv': unif((N_EVEN, 2, 3 * D_B), 0.0, 1.0),
        'rwkv_mu_wa': unif((N_EVEN, 2, RWKV_LORA), 0.0, 1.0),
        'rwkv_w0': unif((N_EVEN, 2, D_B), -2.0, 1.0),
        'rwkv_w_up': nrm((N_EVEN, 2, RWKV_LORA_W, D_B), 0.1 * RWKV_LORA_W ** -0.5),
        'rwkv_a0': nrm((N_EVEN, 2, D_B), 0.5),
        'rwkv_a_up': nrm((N_EVEN, 2, RWKV_LORA_A, D_B), 0.1 * RWKV_LORA_A ** -0.5),
        'rwkv_k_k': 0.85 + nrm((N_EVEN, D_B), 0.02),
        'rwkv_k_a': 1.0 + nrm((N_EVEN, D_B), 0.02),
        'rwkv_r_k': nrm((N_EVEN, RWKV_HEADS, RWKV_HEAD), 0.1),
        'rwkv_gn_w': 1.0 + nrm((N_EVEN, D_B), 0.02),
        'rwkv_gn_b': nrm((N_EVEN, D_B), 0.02),
        'attn_w_in': nrm((N_ODD, D, 3 * D_ATTN), D ** -0.5),
        'attn_w_out': nrm((N_ODD, D_ATTN, D), D_ATTN ** -0.5),
        'attn_q_norm': 1.0 + nrm((N_ODD, ATTN_HEAD_DIM), 0.02),
        'attn_k_norm': 1.0 + nrm((N_ODD, ATTN_HEAD_DIM), 0.02),
    }


def reference(x_prompt, x_sample, ffn1_norm, ffn1_w_in, ffn1_w_out, mix_norm, ffn2_norm, ffn2_w_in,
              ffn2_w_out, block_norm, ab_w_in, ab_w_out, sgu_norm, sgu_w, sgu_b, rwkv_mu_rkv, rwkv_mu_wa,
              rwkv_w0, rwkv_w_up, rwkv_a0, rwkv_a_up, rwkv_k_k, rwkv_k_a, rwkv_r_k, rwkv_gn_w, rwkv_gn_b,
              attn_w_in, attn_w_out, attn_q_norm, attn_k_norm):
    def trunk(x):
        for i in range(DEPTH):
            j = i // 2
            x = x + 0.5 * swiglu_ffn(rmsnorm(x, ffn1_norm[i]), ffn1_w_in[i], ffn1_w_out[i])
            h = rmsnorm(x, mix_norm[i])
            if i % 2 == 0:
                x = x + sgu_rwkv_mixer(h, ab_w_in[j], ab_w_out[j], sgu_norm[j], sgu_w[j], sgu_b[j],
                                       rwkv_mu_rkv[j], rwkv_mu_wa[j], rwkv_w0[j], rwkv_w_up[j],
                                       rwkv_a0[j], rwkv_a_up[j], rwkv_k_k[j], rwkv_k_a[j], rwkv_r_k[j],
                                       rwkv_gn_w[j], rwkv_gn_b[j])
            else:
                x = x + dilated_attention(h, attn_w_in[j], attn_w_out[j], attn_q_norm[j], attn_k_norm[j])
            x = x + 0.5 * swiglu_ffn(rmsnorm(x, ffn2_norm[i]), ffn2_w_in[i], ffn2_w_out[i])
            x = rmsnorm(x, block_norm[i])
        return x

    y_prompt = trunk(x_prompt)
    y_sample = trunk(x_sample)
    return (y_prompt, y_sample)
```

```python
import numpy as np
from contextlib import ExitStack
import concourse.bass as bass
import concourse.mybir as mybir
from concourse.bass_utils import run_bass_kernel_spmd

F32 = mybir.dt.float32
BF16 = mybir.dt.bfloat16
AF = mybir.ActivationFunctionType
ALU = mybir.AluOpType
AX = mybir.AxisListType

D = 1024
FF = 2816
NFC = 22
NKC = 8
EPS = 1e-6
KD = 8


class Res:
    __slots__ = ("lw", "rs")

    def __init__(self):
        self.lw = None
        self.rs = {}


class KB:
    def __init__(self, nc, es):
        self.nc = nc
        self.engs = ("pe", "act", "dve", "pool", "sp")
        self.ops = {e: [] for e in self.engs}
        self.cnt = {e: 0 for e in self.engs}
        self.pending = {e: False for e in self.engs}
        self.waited = {e: {} for e in self.engs}
        self.semh = {}
        for e in ("pe", "act", "dve", "pool"):
            self.semh[e] = es.enter_context(nc.semaphore("s_" + e))
        self.dq = {}
        for q in ("sp", "pool", "act"):
            for j in range(KD):
                self.semh[("d", q, j)] = es.enter_context(nc.semaphore(f"d_{q}_{j}"))
            self.dq[q] = 0
        self.bar = {}

    def _deps(self, reads, writes):
        deps = dict(self.bar)

        def add(k, v):
            if deps.get(k, 0) < v:
                deps[k] = v

        for r in reads:
            if r.lw is not None:
                add(*r.lw)
        for w in writes:
            if w.lw is not None:
                add(*w.lw)
            for k, v in w.rs.items():
                add(k, v)
        return deps

    def _mark(self, me, reads, writes):
        k, v = me
        for r in reads:
            if r.rs.get(k, 0) < v:
                r.rs[k] = v
        for w in writes:
            w.lw = me
            w.rs = {}

    def _waits(self, eng, deps):
        waits = []
        wd = self.waited[eng]
        for k, v in deps.items():
            if k == "pe" and eng == "pe":
                continue
            if wd.get(k, 0) >= v:
                continue
            wd[k] = v
            waits.append((k, v))
        return waits

    def op(self, eng, name, reads=(), writes=(), inc=True, **kw):
        fn = (lambda e: getattr(e, name)(**kw))
        deps = self._deps(reads, writes)
        waits = self._waits(eng, deps)
        if inc:
            self.cnt[eng] += 1
            me = (eng, self.cnt[eng])
            self.pending[eng] = False
            self.ops[eng].append((fn, waits, (eng, 1)))
        else:
            me = (eng, self.cnt[eng] + 1)
            self.pending[eng] = True
            self.ops[eng].append((fn, waits, None))
        self._mark(me, reads, writes)

    def dma(self, q, out, in_, reads=(), writes=(), **kw):
        m = self.dq[q]
        self.dq[q] += 1
        j = m % KD
        val = 16 * (m // KD + 1)
        sk = ("d", q, j)
        deps = self._deps(reads, writes)
        if m >= KD and deps.get(sk, 0) < val - 16:
            deps[sk] = val - 16
        waits = self._waits(q, deps)
        self.ops[q].append((lambda e: e.dma_start(out=out, in_=in_, **kw), waits, (sk, 16)))
        self._mark((sk, val), reads, writes)

    def barrier(self):
        for e in self.engs:
            assert not self.pending[e], e
        b = {}
        for e in ("pe", "act", "dve", "pool"):
            if self.cnt[e]:
                b[e] = self.cnt[e]
        for q, n in self.dq.items():
            for j in range(KD):
                c = (n - j + KD - 1) // KD if n > j else 0
                if c:
                    b[("d", q, j)] = 16 * c
        self.bar = b

    def finish(self):
        self.barrier()
        for e in self.engs:
            waits = self._waits(e, dict(self.bar))
            if waits:
                self.ops[e].append((None, waits, None))

    def emit(self):
        nc = self.nc
        with nc.Block() as block:
            def mk(e):
                def body(eng):
                    for fn, waits, inc in self.ops[e]:
                        for k, v in waits:
                            eng.wait_ge(self.semh[k], v)
                        if fn is None:
                            continue
                        ins = fn(eng)
                        if inc is not None:
                            ins.then_inc(self.semh[inc[0]], inc[1])
                return body
            block.tensor(mk("pe"))
            block.scalar(mk("act"))
            block.vector(mk("dve"))
            block.gpsimd(mk("pool"))
            block.sync(mk("sp"))


class Arena:
    def __init__(self, ap, ncols):
        self.ap = ap
        self.n = ncols
        self.off = 0

    def reset(self):
        self.off = 0

    def f32(self, cols):
        a = self.ap[:, self.off:self.off + cols]
        self.off += cols
        assert self.off <= self.n, ("sbuf arena overflow", self.off, self.n)
        return a

    def bf(self, cols):
        c32 = (cols + 1) // 2
        return self.f32(c32).bitcast(BF16)[:, 0:cols]


def rr(n):
    return [Res() for _ in range(n)]


class Ctx:
    pass


def rstd_ops(kb, ss, rstd, r_ss, r_rstd, n, eps):
    kb.op("act", "activation", [r_ss], [r_rstd], out=rstd, in_=ss, func=AF.Sqrt, scale=1.0 / n, bias=eps)
    kb.op("dve", "reciprocal", [r_rstd], [r_rstd], out=rstd, in_=rstd)


def norm_transpose(kb, C, xs, r_xs, hT, r_hT, col0):
    i = C.nt_i
    C.nt_i += 1
    sq, ss, rstd, hb = C.nt_sq, C.nt_ss[i % 2], C.nt_rstd[i % 2], C.nt_hb[i % 2]
    r_sq, r_ss, r_rstd, r_hb = C.r_nt_sq, C.r_nt_ss[i % 2], C.r_nt_rstd[i % 2], C.r_nt_hb[i % 2]
    kb.op("act", "activation", [r_xs], [r_sq, r_ss], out=sq, in_=xs, func=AF.Square, accum_out=ss)
    rstd_ops(kb, ss, rstd, r_ss, r_rstd, D, EPS)
    if i % 2 == 0:
        kb.op("act", "activation", [r_xs, r_rstd], [r_hb], out=hb, in_=xs, func=AF.Copy, scale=rstd)
    else:
        kb.op("dve", "tensor_scalar", [r_xs, r_rstd], [r_hb], out=hb, in0=xs, scalar1=rstd, scalar2=None,
              op0=ALU.mult)
    pt = C.ps_bf[6 + i % 2].rearrange("p (c m) -> p c m", c=8)
    r_pt = C.r_ps[6 + i % 2]
    for c in range(8):
        kb.op("pe", "transpose", [r_hb, C.r_ident], [r_pt], inc=(c == 7),
              out=pt[:, c, :], in_=hb[:, c * 128:(c + 1) * 128], identity=C.ident)
    if i % 2 == 0:
        kb.op("dve", "tensor_copy", [r_pt], [r_hT], out=hT[:, :, col0:col0 + 128], in_=pt)
    else:
        kb.op("act", "activation", [r_pt], [r_hT], out=hT[:, :, col0:col0 + 128], in_=pt, func=AF.Copy)
    return rstd, r_rstd


def alloc_norm_tmps(C, A):
    C.nt_i = 0
    C.nt_sq = A.bf(1024)
    C.r_nt_sq = Res()
    C.nt_ss = [A.f32(1), A.f32(1)]
    C.r_nt_ss = rr(2)
    C.nt_rstd = [A.f32(1), A.f32(1)]
    C.r_nt_rstd = rr(2)
    hb0 = A.bf(1024)
    C.nt_hb = [hb0, hb0]
    r0 = Res()
    C.r_nt_hb = [r0, r0]


def load_weight(kb, C, dst, r_dst, src_rows, ncols, scale, extra=(), piece=704, parts=128):
    for c0 in range(0, ncols, piece):
        n = min(piece, ncols - c0)
        i = C.wl_i
        C.wl_i += 1
        st, r_st = C.wl_st[i % 2], C.r_wl_st[i % 2]
        kb.dma("sp", st[0:parts, 0:n], src_rows[:, c0:c0 + n], [], [r_st])
        eng = ("act", "dve")[i % 2]
        if eng == "act":
            kb.op("act", "activation", [r_st] + list(extra), [r_dst], out=dst[:, c0:c0 + n], in_=st[0:parts, 0:n],
                  func=AF.Copy, scale=scale)
        else:
            kb.op(eng, "tensor_scalar", [r_st] + list(extra), [r_dst], out=dst[:, c0:c0 + n], in0=st[0:parts, 0:n],
                  scalar1=scale, scalar2=None, op0=ALU.mult)


def alloc_wl(C, A, piece=704):
    C.wl_i = 0
    C.wl_st = [A.f32(piece), A.f32(piece)]
    C.r_wl_st = rr(2)


def load_gcols(kb, A, g_d, n):
    gt = A.f32(n)
    r = Res()
    kb.dma("sp", gt, g_d.rearrange("(c p) -> p c", p=128), [], [r], allow_slow_non_contiguous=True)
    return gt, r


def phase_begin(kb, C):
    A = C.A
    kb.barrier()
    A.reset()
    C.ident = A.bf(128)
    C.r_ident = Res()
    idf = A.f32(128)
    r_idf = Res()
    kb.dma("sp", idf, C.ident_d, [], [r_idf])
    kb.op("dve", "tensor_copy", [r_idf], [C.r_ident], out=C.ident, in_=idf)
    return A


def phase_ffn(kb, C, T, x_in, x_out, w_in, w_out, g_d, gfin_d):
    A = phase_begin(kb, C)
    W1 = A.bf(8 * 2 * FF).rearrange("p (c n) -> p c n", c=8)
    r_W1 = rr(8)
    W2 = A.bf(NFC * D).rearrange("p (c n) -> p c n", c=NFC)
    r_W2 = rr(NFC)
    gt, r_gt = load_gcols(kb, A, g_d, 8)
    alloc_wl(C, A)
    for c in range(8):
        load_weight(kb, C, W1[:, c, :], r_W1[c], w_in[c * 128:(c + 1) * 128, :], 2 * FF, gt[:, c:c + 1],
                    extra=[r_gt])
    for f in range(NFC):
        load_weight(kb, C, W2[:, f, :], r_W2[f], w_out[f * 128:(f + 1) * 128, :], D, 0.5)
    gfin = None
    r_gfin = Res()
    if gfin_d is not None:
        gfin = A.f32(D)
        kb.dma("sp", gfin, gfin_d.partition_broadcast(128), [], [r_gfin])
    alloc_norm_tmps(C, A)
    NXS = 2
    xs = [A.f32(D) for _ in range(NXS)]
    r_xs = rr(NXS)
    xr = [A.f32(D) for _ in range(2)]
    r_xr = rr(2)
    xo = [A.f32(D) for _ in range(2)]
    r_xo = rr(2)
    hT0 = A.bf(8 * 512).rearrange("p (c n) -> p c n", c=8)
    hT = [hT0, hT0]
    r_hT0 = Res()
    r_hT = [r_hT0, r_hT0]
    aT = A.bf(NFC * 512).rearrange("p (c n) -> p c n", c=NFC)
    r_aT = rr(NFC)
    sg = [A.bf(512) for _ in range(2)]
    r_sg = rr(2)
    fs_ss = [A.f32(1) for _ in range(2)]
    r_fs_ss = rr(2)
    fs_rstd = [A.f32(1) for _ in range(2)]
    r_fs_rstd = rr(2)
    fs_sq = C.nt_sq
    r_fs_sq = C.r_nt_sq
    ps, r_ps = C.ps, C.r_ps
    ntile = T // 512
    xi = 0
    gi = 0
    oi = 0
    for t in range(ntile):
        h, r_h = hT[t % 2], r_hT[t % 2]
        for j in range(4):
            r0 = t * 512 + j * 128
            b = xi % NXS
            xi += 1
            kb.dma("sp", xs[b], x_in[r0:r0 + 128, :], [], [r_xs[b]])
            norm_transpose(kb, C, xs[b], r_xs[b], h, r_h, j * 128)
        for f in range(NFC):
            pg, pu = 2 * (gi % 2), 2 * (gi % 2) + 1
            gi += 1
            for c in range(8):
                kb.op("pe", "matmul", [r_W1[c], r_h], [r_ps[pg]], inc=(c == 7),
                      out=ps[pg], lhsT=W1[:, c, f * 128:(f + 1) * 128], rhs=h[:, c, :],
                      start=(c == 0), stop=(c == 7))
            for c in range(8):
                kb.op("pe", "matmul", [r_W1[c], r_h], [r_ps[pu]], inc=(c == 7),
                      out=ps[pu], lhsT=W1[:, c, FF + f * 128:FF + (f + 1) * 128], rhs=h[:, c, :],
                      start=(c == 0), stop=(c == 7))
            s, r_s = sg[f % 2], r_sg[f % 2]
            kb.op("act", "activation", [r_ps[pg]], [r_s], out=s, in_=ps[pg], func=AF.Silu)
            kb.op("dve", "tensor_tensor", [r_s, r_ps[pu]], [r_aT[f]], out=aT[:, f, :], in0=s, in1=ps[pu],
                  op=ALU.mult)
        for j in range(4):
            r0 = t * 512 + j * 128
            b = oi % 2
            oi += 1
            kb.dma("sp", xr[b], x_in[r0:r0 + 128, :], [], [r_xr[b]])
            for hf in range(2):
                py = 4 + hf
                for f in range(NFC):
                    kb.op("pe", "matmul", [r_aT[f], r_W2[f]], [r_ps[py]], inc=(f == NFC - 1),
                          out=ps[py], lhsT=aT[:, f, j * 128:(j + 1) * 128], rhs=W2[:, f, hf * 512:(hf + 1) * 512],
                          start=(f == 0), stop=(f == NFC - 1))
                kb.op("dve", "tensor_tensor", [r_ps[py], r_xr[b]], [r_xo[b]],
                      out=xo[b][:, hf * 512:(hf + 1) * 512], in0=ps[py], in1=xr[b][:, hf * 512:(hf + 1) * 512],
                      op=ALU.add)
            if gfin_d is not None:
                ss, rstd = fs_ss[b], fs_rstd[b]
                kb.op("act", "activation", [r_xo[b]], [r_fs_sq, r_fs_ss[b]], out=fs_sq, in_=xo[b],
                      func=AF.Square, accum_out=ss)
                rstd_ops(kb, ss, rstd, r_fs_ss[b], r_fs_rstd[b], D, EPS)
                kb.op("dve", "scalar_tensor_tensor", [r_xo[b], r_fs_rstd[b], r_gfin], [r_xo[b]],
                      out=xo[b], in0=xo[b], scalar=rstd, in1=gfin, op0=ALU.mult, op1=ALU.mult)
            kb.dma("pool", x_out[r0:r0 + 128, :], xo[b], [r_xo[b]], [])


CONST_SHAPES = {"bones": [128, 128], "hind": [128, 32], "m12f": [128, 128], "m12b": [128, 128],
                "maf": [64, 64], "mab": [64, 64], "cmask": [128, 512], "bdmask": [128, 512]}


def make_consts(T, seg):
    c = {}
    p = np.arange(128)
    c["bones"] = (p[:, None] // 64 == p[None, :] // 64).astype(np.float32)
    hind = np.zeros((128, 4, 8), np.float32)
    for i in range(4):
        for e in range(2):
            hind[64 * e:64 * e + 64, i, 2 * i + e] = 1.0
    c["hind"] = hind.reshape(128, 32)
    j = p[:, None] % 64
    tt = p[None, :]
    c["m12f"] = np.where(tt < 64, j < tt, j <= tt - 64).astype(np.float32)
    c["m12b"] = np.where(tt < 64, j > tt, j >= tt - 64).astype(np.float32)
    q = np.arange(64)
    c["maf"] = (q[None, :] < q[:, None]).astype(np.float32)
    c["mab"] = (q[None, :] > q[:, None]).astype(np.float32)
    cm = np.ones((128, 512), np.float32)
    cm[:, ::64] = 0.0
    c["cmask"] = cm
    bd = np.zeros((128, 4, 128), np.float32)
    bd[0:64, :, 0:64] = 1.0
    bd[64:128, :, 64:128] = 1.0
    c["bdmask"] = bd.reshape(128, 512)
    NT = T // 512
    NCH = T // 64
    tf = np.ones((2, NT), np.float32)
    for t in range(NT):
        if (t * 512) % seg == 0:
            tf[0, t] = 0.0
        if (t * 512 + 512) % seg == 0:
            tf[1, t] = 0.0
    c["tflags"] = tf
    cf = np.ones((NCH,), np.float32)
    cb = np.ones((NCH,), np.float32)
    for ch in range(NCH):
        if (ch * 64) % seg == 0:
            cf[ch] = 0.0
        if (ch * 64 + 64) % seg == 0:
            cb[ch] = 0.0
    c["cflagf"] = cf
    c["cflagb"] = cb
    return c


def g_d_bc(g_d):
    return g_d.partition_broadcast(128)


def build(T, stage=99):
    nc = bass.Bass("TRN2", target_bir_lowering=False)
    dt = lambda name, shape, kind="ExternalInput": nc.dram_tensor(name, list(shape), F32, kind=kind).ap()
    x = dt("x", [T, D])
    y = dt("y", [T, D], "ExternalOutput")
    ident_d = dt("ident", [128, 128])
    wd = {}
    for nm, shp in (("ffn1_norm", [2, D]), ("ffn1_w_in", [2, D, 2 * FF]), ("ffn1_w_out", [2, FF, D]),
                    ("mix_norm", [2, D]), ("ffn2_norm", [2, D]), ("ffn2_w_in", [2, D, 2 * FF]),
                    ("ffn2_w_out", [2, FF, D]), ("block_norm", [2, D])):
        wd[nm] = dt(nm, shp)
    for nm, shp in (("ab_w_in", [1, D, DAB]), ("ab_w_out", [1, D, D]), ("sgu_norm", [1, 512]),
                    ("sgu_w", [1, 4, 128, 128]), ("sgu_b", [1, 4, 128]), ("rwkv_mu_rkv", [1, 2, 1536]),
                    ("rwkv_mu_wa", [1, 2, 128]), ("rwkv_w0", [1, 2, 512]), ("rwkv_w_up", [1, 2, 64, 512]),
                    ("rwkv_a0", [1, 2, 512]), ("rwkv_a_up", [1, 2, 64, 512]), ("rwkv_k_k", [1, 512]),
                    ("rwkv_k_a", [1, 512]), ("rwkv_r_k", [1, 8, 64]), ("rwkv_gn_w", [1, 512]),
                    ("rwkv_gn_b", [1, 512]), ("attn_w_in", [1, D, 3 * D]), ("attn_w_out", [1, D, D]),
                    ("attn_q_norm", [1, 64]), ("attn_k_norm", [1, 64])):
        wd[nm] = dt(nm, shp)
    cd = {}
    for nm, shp in CONST_SHAPES.items():
        cd[nm] = dt(nm, shp)
    cd["tflags"] = dt("tflags", [2, T // 512])
    cd["cflagf"] = dt("cflagf", [T // 64])
    cd["cflagb"] = dt("cflagb", [T // 64])
    xa = dt("xa", [T, D], "Internal")
    xb = dt("xb", [T, D], "Internal")
    for nm, shp in ATT_SHAPES.items():
        cd[nm] = dt(nm, shp)
    cd["vfl"] = dt("vfl", [128, (T // 512) * NVT])
    QT = nc.dram_tensor("qt_s", [D, T], BF16, kind="Internal").ap()
    KT = nc.dram_tensor("kt_s", [D, T + 2 * PAD], BF16, kind="Internal").ap()
    VD = nc.dram_tensor("vd_s", [T + 2 * PAD, 1040], BF16, kind="Internal").ap()
    dbg = stage in (3, 4)
    YD = [dt(f"yd{i}", [T, 512], "ExternalOutput" if dbg else "Internal") for i in range(2)]
    BD = [dt(f"bd{i}", [T, 512], "ExternalOutput" if dbg else "Internal") for i in range(2)]
    with ExitStack() as es:
        arena = es.enter_context(nc.sbuf_tensor("arena", [128, 53200], F32))
        C = Ctx()
        C.A = Arena(arena, 53200)
        C.ident_d = ident_d
        C.cd = cd
        C.ps = []
        C.ps_bf = []
        C.r_ps = rr(8)
        for i in range(8):
            p = es.enter_context(nc.psum_tensor(f"ps{i}", [128, 512], F32))
            C.ps.append(p[:, :])
            C.ps_bf.append(p[:, :].bitcast(BF16))
        kb = KB(nc, es)
        if stage == 1:
            phase_ffn(kb, C, T, x, y, wd["ffn1_w_in"][0], wd["ffn1_w_out"][0], wd["ffn1_norm"][0], None)
        elif stage == 3:
            phase_ffn(kb, C, T, x, xa, wd["ffn1_w_in"][0], wd["ffn1_w_out"][0], wd["ffn1_norm"][0], None)
            phase_rwkv(kb, C, T, 0, xa, wd, YD[0], BD[0])
            phase_rwkv(kb, C, T, 1, xa, wd, YD[1], BD[1])
        elif stage == 4:
            phase_ffn(kb, C, T, x, xa, wd["ffn1_w_in"][0], wd["ffn1_w_out"][0], wd["ffn1_norm"][0], None)
            phase_rwkv(kb, C, T, 0, xa, wd, YD[0], BD[0])
            phase_rwkv(kb, C, T, 1, xa, wd, YD[1], BD[1])
            phase_mixc(kb, C, T, xa, y, wd, YD, BD)
        elif stage == 2:
            phase_ffn(kb, C, T, x, xa, wd["ffn1_w_in"][0], wd["ffn1_w_out"][0], wd["ffn1_norm"][0], None)
            phase_ffn(kb, C, T, xa, y, wd["ffn2_w_in"][0], wd["ffn2_w_out"][0], wd["ffn2_norm"][0],
                      wd["block_norm"][0])
        else:
            phase_ffn(kb, C, T, x, xa, wd["ffn1_w_in"][0], wd["ffn1_w_out"][0], wd["ffn1_norm"][0], None)
            phase_rwkv(kb, C, T, 0, xa, wd, YD[0], BD[0])
            phase_rwkv(kb, C, T, 1, xa, wd, YD[1], BD[1])
            phase_mixc(kb, C, T, xa, xb, wd, YD, BD)
            phase_ffn(kb, C, T, xb, xa, wd["ffn2_w_in"][0], wd["ffn2_w_out"][0], wd["ffn2_norm"][0],
                      wd["block_norm"][0] if True else None)
            if stage == 5:
                pass
            phase_ffn(kb, C, T, xa, xb, wd["ffn1_w_in"][1], wd["ffn1_w_out"][1], wd["ffn1_norm"][1], None)
            phase_attn_qkv(kb, C, T, xb, wd, QT, KT, VD)
            phase_attn(kb, C, T, xb, xa if stage != 7 else y, wd, QT, KT, VD)
            if stage != 7:
                phase_ffn(kb, C, T, xa, y, wd["ffn2_w_in"][1], wd["ffn2_w_out"][1], wd["ffn2_norm"][1],
                          wd["block_norm"][1])
        kb.finish()
        kb.emit()
    return nc


WNAMES = ("ffn1_norm", "ffn1_w_in", "ffn1_w_out", "mix_norm", "ffn2_norm", "ffn2_w_in", "ffn2_w_out",
          "block_norm", "ab_w_in", "ab_w_out", "sgu_norm", "sgu_w", "sgu_b", "rwkv_mu_rkv", "rwkv_mu_wa",
          "rwkv_w0", "rwkv_w_up", "rwkv_a0", "rwkv_a_up", "rwkv_k_k", "rwkv_k_a", "rwkv_r_k", "rwkv_gn_w",
          "rwkv_gn_b", "attn_w_in", "attn_w_out", "attn_q_norm", "attn_k_norm")


def run_streams(streams, weights, stage=99, segs=None, raw=False):
    T = streams[0].shape[0]
    nc = build(T, stage)
    ident = np.eye(128, dtype=np.float32)
    in_maps = []
    for si, s in enumerate(streams):
        m = {"x": np.ascontiguousarray(s), "ident": ident}
        for k in WNAMES:
            m[k] = weights[k]
        m.update(make_consts(T, segs[si] if segs is not None else T))
        m.update(attn_consts(T, segs[si] if segs is not None else T))
        in_maps.append(m)
    res = run_bass_kernel_spmd(nc, in_maps, core_ids=list(range(len(streams))))
    if raw:
        return res.results
    return [r["y"] for r in res.results]


def kernel(**inputs):
    xp = np.asarray(inputs["x_prompt"], dtype=np.float32)
    xsm = np.asarray(inputs["x_sample"], dtype=np.float32)
    weights = {k: np.ascontiguousarray(np.asarray(v, dtype=np.float32)) for k, v in inputs.items()
               if k not in ("x_prompt", "x_sample")}
    B, S, _ = xp.shape
    TT = xsm.shape[1]
    per = TT // S
    streams = [xsm[0]]
    for i in range(B // per):
        streams.append(xp[i * per:(i + 1) * per].reshape(TT, D))
    while len(streams) < 8:
        streams.append(streams[1])
    segs = [TT] + [S] * (len(streams) - 1)
    outs = run_streams(streams, weights, segs=segs)
    y_sample = outs[0].reshape(1, TT, D)
    y_prompt = np.concatenate([outs[1 + i].reshape(per, S, D) for i in range(B // per)], axis=0)
    return (y_prompt, y_sample)


DAB = 3328
import os
STOP = int(os.environ.get('KSTOP', '0'))
LOGC = -0.6065306597126334


def v3(ap, a):
    return ap.rearrange("p (a b) -> p a b", a=a)


def v4(ap, a, b):
    return ap.rearrange("p (a b c) -> p a b c", a=a, b=b)


def cload(kb, A, src, cols, bf=False, parts=128):
    t32 = A.f32(cols)
    r = Res()
    kb.dma("sp", t32[0:parts, :], src, [], [r])
    if not bf:
        return t32, r
    tb = A.bf(cols)
    rb = Res()
    kb.op("dve", "tensor_copy", [r], [rb], out=tb[0:parts, :], in_=t32[0:parts, :])
    return tb, rb


class PsPool:
    def __init__(self, banks):
        self.banks = banks
        self.i = 0

    def get(self):
        b = self.banks[self.i % len(self.banks)]
        self.i += 1
        return b


def phase_rwkv(kb, C, T, d, x_in, W, YD, BD):
    A = phase_begin(kb, C)
    ps, psb, r_ps = C.ps, C.ps_bf, C.r_ps
    ident, r_ident = C.ident, C.r_ident
    NT = T // 512
    NCH = T // 64
    gp = PsPool([0, 1, 2, 3])
    ev = [0]

    def evac(out, in_, reads, writes, scale=None):
        ev[0] += 1
        if ev[0] % 2 == 0:
            if scale is None:
                kb.op("act", "activation", reads, writes, out=out, in_=in_, func=AF.Copy)
            else:
                kb.op("act", "activation", reads, writes, out=out, in_=in_, func=AF.Copy, scale=scale)
        else:
            if scale is None:
                kb.op("dve", "tensor_copy", reads, writes, out=out, in_=in_)
            else:
                kb.op("dve", "tensor_scalar", reads, writes, out=out, in0=in_, scalar1=scale, scalar2=None,
                      op0=ALU.mult)

    bones, r_bones = cload(kb, A, C.cd["bones"], 128, bf=True)
    hind, r_hind = cload(kb, A, C.cd["hind"], 32, bf=True)
    m12, r_m12 = cload(kb, A, C.cd["m12f" if d == 0 else "m12b"], 128, bf=True)
    ma, r_ma = cload(kb, A, C.cd["maf" if d == 0 else "mab"], 64, bf=True, parts=64)
    cmask, r_cmask = cload(kb, A, C.cd["cmask"], 512)
    bdm, r_bdm = cload(kb, A, C.cd["bdmask"], 512, bf=True)
    i64 = ident[0:64, 0:64]
    tfl = A.f32(NT)
    r_tfl = Res()
    kb.dma("sp", tfl[0:2, :], C.cd["tflags"], [], [r_tfl])
    cfl = A.f32(NCH)
    r_cfl = Res()
    kb.dma("sp", cfl, C.cd["cflagf" if d == 0 else "cflagb"].partition_broadcast(128), [], [r_cfl])
    gt, r_gt = load_gcols(kb, A, W["mix_norm"][0], 8)
    mu_r, r_mu_r = load_gcols(kb, A, W["rwkv_mu_rkv"][0, d], 12)
    mu_w, r_mu_w = load_gcols(kb, A, W["rwkv_mu_wa"][0, d], 1)
    w0c, r_w0c = load_gcols(kb, A, W["rwkv_w0"][0, d], 4)
    a0c, r_a0c = load_gcols(kb, A, W["rwkv_a0"][0, d], 4)
    kkc, r_kkc = load_gcols(kb, A, W["rwkv_k_k"][0], 4)
    kac, r_kac = load_gcols(kb, A, W["rwkv_k_a"][0], 4)
    rkc, r_rkc = load_gcols(kb, A, W["rwkv_r_k"][0].rearrange("h n -> (h n)"), 4)
    r_cc = [r_w0c, r_a0c, r_kkc, r_kac, r_rkc, r_mu_r, r_mu_w]
    NW = 1664
    Wp = v3(A.bf(8 * NW), 8)
    r_Wp = rr(8)
    alloc_wl(C, A)
    wab = W["ab_w_in"][0]
    wa0 = 3072 + d * 128
    for c in range(8):
        rows = wab[c * 128:(c + 1) * 128, :]
        load_weight(kb, C, Wp[:, c, 0:1536], r_Wp[c], rows[:, 1024:2560], 1536, gt[:, c:c + 1], extra=[r_gt])
        load_weight(kb, C, Wp[:, c, 1536:NW], r_Wp[c], rows[:, wa0:wa0 + 128], 128, gt[:, c:c + 1], extra=[r_gt])
    WUP = A.bf(512)
    r_WUP = Res()
    AUP = A.bf(512)
    r_AUP = Res()
    st, r_st = C.wl_st[0], C.r_wl_st[0]
    kb.dma("sp", st[0:64, 0:512], W["rwkv_w_up"][0, d], [], [r_st])
    kb.op("dve", "tensor_copy", [r_st], [r_WUP], out=WUP[0:64, :], in_=st[0:64, 0:512])
    st, r_st = C.wl_st[1], C.r_wl_st[1]
    kb.dma("sp", st[64:128, 0:512], W["rwkv_a_up"][0, d], [], [r_st])
    kb.op("dve", "tensor_copy", [r_st], [r_AUP], out=AUP[64:128, :], in_=st[64:128, 0:512])

    alloc_norm_tmps(C, A)
    xs = [A.f32(D) for _ in range(2)]
    r_xs = rr(2)
    hx = xs[0]
    r_hx = r_xs[0]
    hxb = C.nt_hb[0]
    r_hxb = C.r_nt_hb[0]
    hx_ss = A.f32(1)
    hx_rs = A.f32(1)
    r_hxs = Res()
    hT = v3(A.bf(8 * 514), 8)
    r_hT = Res()
    Pb = [A.f32(514) for _ in range(2)]
    r_Pb = rr(2)
    Rr = A.f32(512)
    Kk = A.f32(512)
    r_Rr, r_Kk = Res(), Res()
    TWD = A.bf(512)
    AD = A.bf(512)
    r_TWD, r_AD = Res(), Res()
    tmp = [A.f32(512) for _ in range(11)]
    r_tmp = rr(11)
    SQ = A.bf(512)
    PRb = A.bf(512)
    r_SQ, r_PRb = Res(), Res()
    VTp = [A.bf(576) for _ in range(4)]
    r_VTp = rr(4)
    AK = [v3(A.bf(1024), 8) for _ in range(4)]
    KR = [v3(A.bf(1024), 8) for _ in range(4)]
    r_AK, r_KR = rr(4), rr(4)
    Kc = [v3(A.bf(1024), 8) for _ in range(4)]
    r_Kc = rr(4)
    KTS = [v3(A.bf(1024), 8) for _ in range(2)]
    r_KTS = rr(2)
    M12 = [[v3(A.bf(1024), 8) for _ in range(2)] for _ in range(4)]
    r_M12 = [rr(2) for _ in range(4)]
    LW = [v3(A.bf(512), 8) for _ in range(4)]
    r_LW = rr(4)
    X = v4(A.bf(8 * 4 * 128), 8, 4)
    r_XU = rr(8)
    r_XV = rr(4)
    U0 = v4(A.bf(8 * 4 * 128), 8, 4)
    r_U0 = rr(4)
    DG = v4(A.bf(4 * 8 * 128), 4, 8)
    r_DG = Res()
    QsS = [[v4(A.bf(1024), 8, 2) for _ in range(2)] for _ in range(2)]
    QTsS = [[v4(A.bf(1024), 8, 2) for _ in range(2)] for _ in range(2)]
    r_QsS, r_QTsS = [rr(2), rr(2)], [rr(2), rr(2)]
    PTS = [v4(A.bf(1024), 8, 2) for _ in range(2)]
    r_PTS = rr(2)
    BVsS = [v4(A.bf(1024), 8, 2) for _ in range(2)]
    r_BVsS = rr(2)
    Rbd = [v3(A.bf(512), 4) for _ in range(2)]
    r_Rbd = rr(2)
    YT = [C.wl_st[0][:, 0:512], C.wl_st[1][:, 0:512]]
    r_YT = C.r_wl_st
    GAMALL = v3(A.f32(4 * (NCH + 1)), 4)
    r_GAM = Res()
    GP = v3(A.f32(32), 4)
    r_GP = Res()
    rkS = A.f32(32)
    r_rkS = Res()
    bo0 = A.f32(512)
    BO = [bo0, bo0]
    r_bo0 = Res()
    r_BO = [r_bo0, r_bo0]
    print('rwkv arena cols', A.off)

    kb.op("pool", "memset", [], [r_GAM], ap=GAMALL, constant=0.0)
    for b in range(2):
        kb.op("pool", "memset", [], [r_Rbd[b]], ap=Rbd[b], constant=0.0)
    for i in range(4):
        kb.op("pool", "memset", [], [r_VTp[i]], ap=VTp[i], constant=0.0)
    kb.op("pool", "memset", [], [r_hT], ap=hT, constant=0.0)

    tiles = list(range(NT)) if d == 0 else list(range(NT - 1, -1, -1))
    xi = 0
    yi = 0
    cur = 0
    for t in tiles:
        t0 = t * 512
        kb.op("pool", "memset", [], [r_hx], ap=hx[0:2, :], constant=0.0)
        if t0 - 1 >= 0:
            kb.dma("sp", hx[0:1, :], x_in[t0 - 1:t0, :], [], [r_hx])
        if t0 + 512 < T:
            kb.dma("sp", hx[1:2, :], x_in[t0 + 512:t0 + 513, :], [], [r_hx])
        kb.op("act", "activation", [r_hx], [r_hxb, r_hxs], out=hxb[0:2, :], in_=hx[0:2, :], func=AF.Square,
              accum_out=hx_ss[0:2, :])
        kb.op("act", "activation", [r_hxs], [r_hxs], out=hx_rs[0:2, :], in_=hx_ss[0:2, :], func=AF.Sqrt,
              scale=1.0 / D, bias=EPS)
        kb.op("dve", "reciprocal", [r_hxs], [r_hxs], out=hx_rs[0:2, :], in_=hx_rs[0:2, :])
        kb.op("dve", "tensor_tensor", [r_hxs, r_tfl], [r_hxs], out=hx_rs[0:2, :], in0=hx_rs[0:2, :],
              in1=tfl[0:2, t:t + 1], op=ALU.mult)
        kb.op("dve", "tensor_scalar", [r_hx, r_hxs], [r_hxb], out=hxb[0:2, :], in0=hx[0:2, :],
              scalar1=hx_rs[0:2, :], scalar2=None, op0=ALU.mult)
        b = gp.get()
        pth = v3(psb[b][:, 0:16], 8)
        for c in range(8):
            kb.op("pe", "transpose", [r_hxb, r_ident], [r_ps[b]], inc=(c == 7),
                  out=pth[:, c, :], in_=hxb[0:2, c * 128:(c + 1) * 128], identity=ident[0:2, 0:2])
        kb.op("dve", "tensor_copy", [r_ps[b]], [r_hT], out=hT[:, :, 0:1], in_=pth[:, :, 0:1])
        kb.op("dve", "tensor_copy", [r_ps[b]], [r_hT], out=hT[:, :, 513:514], in_=pth[:, :, 1:2])
        for j in range(4):
            r0 = t0 + j * 128
            bx = xi % 2
            xi += 1
            kb.dma("sp", xs[bx], x_in[r0:r0 + 128, :], [], [r_xs[bx]])
            norm_transpose(kb, C, xs[bx], r_xs[bx], hT, r_hT, 1 + j * 128)

        sh0 = 0 if d == 0 else 2

        def proj(ft, mucol, r_mu):
            bm = gp.get()
            for c in range(8):
                kb.op("pe", "matmul", [r_Wp[c], r_hT], [r_ps[bm]], inc=(c == 7), out=ps[bm],
                      lhsT=Wp[:, c, ft * 128:(ft + 1) * 128], rhs=hT[:, c, 1:513], start=(c == 0), stop=(c == 7))
            bh = gp.get()
            for c in range(8):
                kb.op("pe", "matmul", [r_Wp[c], r_hT], [r_ps[bh]], inc=(c == 7), out=ps[bh][:, 0:2],
                      lhsT=Wp[:, c, ft * 128:(ft + 1) * 128], rhs=hT[:, c, 0:514:513], start=(c == 0),
                      stop=(c == 7))
            k = proj.i % 2
            proj.i += 1
            P_, r_P = Pb[k], r_Pb[k]
            evac(P_[:, 1:513], ps[bm], [r_ps[bm]], [r_P])
            kb.op("dve", "tensor_copy", [r_ps[bh]], [r_P], out=P_[:, 0:514:513], in_=ps[bh][:, 0:2])
            return P_, r_P, mucol, r_mu

        proj.i = 0

        def shiftmix(out, r_out, pr):
            P_, r_P, mucol, r_mu = pr
            dd, r_dd = tmp[9], r_tmp[9]
            kb.op("pool", "tensor_tensor", [r_P], [r_dd], out=dd, in0=P_[:, sh0:sh0 + 512], in1=P_[:, 1:513],
                  op=ALU.subtract)
            kb.op("dve", "scalar_tensor_tensor", [r_dd, r_P, r_mu], [r_out], out=out, in0=dd, scalar=mucol,
                  in1=P_[:, 1:513], op0=ALU.mult, op1=ALU.add)

        pr = proj(12, mu_w[:, 0:1], r_mu_w)
        zwa, r_zwa = tmp[0], r_tmp[0]
        shiftmix(zwa, r_zwa, pr)
        kb.op("act", "activation", [r_zwa], [r_TWD], out=TWD[0:64, :], in_=zwa[0:64, :], func=AF.Tanh)
        kb.op("pool", "tensor_copy", [r_zwa], [r_AD], out=AD[64:128, :], in_=zwa[64:128, :])

        brk = 4
        stq = {}

        def pre(i):
            pr = proj(i, mu_r[:, i:i + 1], r_mu_r)
            shiftmix(Rr, r_Rr, pr)
            pr = proj(4 + i, mu_r[:, 4 + i:5 + i], r_mu_r)
            shiftmix(Kk, r_Kk, pr)
            pr = proj(8 + i, mu_r[:, 8 + i:9 + i], r_mu_r)
            shiftmix(VTp[i][:, 64:576], r_VTp[i], pr)
            SIG, Aa, KK, RN, L, LX, EP, EM, EX = tmp[0:9]
            r_SIG, r_Aa, r_KK, r_RN, r_L, r_LX, r_EP, r_EM, r_EX = r_tmp[0:9]
            bw = gp.get()
            kb.op("pe", "matmul", [r_WUP, r_TWD], [r_ps[bw]], out=ps[bw], lhsT=WUP[0:64, i * 128:(i + 1) * 128],
                  rhs=TWD[0:64, :], start=True, stop=True)
            kb.op("act", "activation", [r_ps[bw], r_w0c], [r_SIG], out=SIG, in_=ps[bw], func=AF.Sigmoid,
                  bias=w0c[:, i:i + 1])
            kb.op("pool", "tensor_scalar", [r_SIG], [r_SIG], out=SIG, in0=SIG, scalar1=LOGC, scalar2=None,
                  op0=ALU.mult)
            ba = gp.get()
            kb.op("pe", "matmul", [r_AUP, r_AD], [r_ps[ba]], out=ps[ba], lhsT=AUP[64:128, i * 128:(i + 1) * 128],
                  rhs=AD[64:128, :], start=True, stop=True)
            kb.op("act", "activation", [r_ps[ba], r_a0c], [r_Aa], out=Aa, in_=ps[ba], func=AF.Sigmoid,
                  bias=a0c[:, i:i + 1])
            kb.op("dve", "tensor_scalar", [r_Kk, r_kkc], [r_KK], out=KK, in0=Kk, scalar1=kkc[:, i:i + 1],
                  scalar2=None, op0=ALU.mult)
            kb.op("act", "activation", [r_KK], [r_SQ], out=SQ, in_=KK, func=AF.Square)
            bn = gp.get()
            kb.op("pe", "matmul", [r_bones, r_SQ], [r_ps[bn]], out=ps[bn], lhsT=bones, rhs=SQ, start=True, stop=True)
            kb.op("act", "activation", [r_ps[bn]], [r_RN], out=RN, in_=ps[bn], func=AF.Sqrt, bias=1e-24)
            kb.op("dve", "reciprocal", [r_RN], [r_RN], out=RN, in_=RN)
            kb.op("dve", "tensor_tensor", [r_KK, r_RN], [r_KK], out=KK, in0=KK, in1=RN, op=ALU.mult)
            if d == 0:
                kb.op("dve", "tensor_tensor_scan", [r_SIG, r_cmask], [r_L], out=L, data0=cmask, data1=SIG,
                      initial=0.0, op0=ALU.mult, op1=ALU.add)
            else:
                Lf, r_Lf = tmp[10], r_tmp[10]
                kb.op("dve", "tensor_tensor_scan", [r_SIG, r_cmask], [r_Lf], out=Lf, data0=cmask, data1=SIG,
                      initial=0.0, op0=ALU.mult, op1=ALU.add)
                kb.op("pool", "tensor_tensor", [r_SIG, r_Lf], [r_LX], out=LX, in0=SIG, in1=Lf, op=ALU.subtract)
                kb.op("dve", "tensor_tensor", [r_LX, r_Lf], [r_L], out=v3(L, 8), in0=v3(LX, 8),
                      in1=v3(Lf, 8)[:, :, 63:64].to_broadcast([128, 8, 64]), op=ALU.add)
            kb.op("pool", "tensor_tensor", [r_L, r_SIG], [r_LX], out=LX, in0=L, in1=SIG, op=ALU.subtract)
            kb.op("act", "activation", [r_L], [r_EP], out=EP, in_=L, func=AF.Exp)
            kb.op("act", "activation", [r_L], [r_EM], out=EM, in_=L, func=AF.Exp, scale=-1.0)
            kb.op("act", "activation", [r_LX], [r_EX], out=EX, in_=LX, func=AF.Exp)
            gcol = 63 if d == 0 else 0
            gdst = GAMALL[:, i, t * 8 + 1:t * 8 + 9] if d == 0 else GAMALL[:, i, t * 8:t * 8 + 8]
            kb.op("pool", "tensor_copy", [r_EP], [r_GAM], out=gdst, in_=v3(EP, 8)[:, :, gcol])
            kb.op("dve", "tensor_tensor", [r_KK, r_EX], [r_KR[i]], out=KR[i][:, :, 0:64], in0=v3(KK, 8),
                  in1=v3(EX, 8), op=ALU.mult)
            kb.op("dve", "tensor_tensor", [r_Rr, r_EP], [r_KR[i]], out=KR[i][:, :, 64:128], in0=v3(Rr, 8),
                  in1=v3(EP, 8), op=ALU.mult)
            kb.op("pool", "tensor_tensor", [r_KK, r_Aa], [r_RN], out=RN, in0=KK, in1=Aa, op=ALU.mult)
            kb.op("dve", "tensor_tensor", [r_RN, r_EM], [r_AK[i]], out=AK[i][:, :, 0:64], in0=v3(RN, 8),
                  in1=v3(EM, 8), op=ALU.mult)
            kb.op("dve", "tensor_scalar", [r_Aa, r_kac], [r_Aa], out=Aa, in0=Aa, scalar1=-1.0,
                  scalar2=kac[:, i:i + 1], op0=ALU.add, op1=ALU.mult)
            kb.op("dve", "scalar_tensor_tensor", [r_Aa, r_Kk], [r_LX], out=LX, in0=Aa, scalar=1.0, in1=Kk,
                  op0=ALU.add, op1=ALU.mult)
            kb.op("dve", "tensor_tensor", [r_LX, r_EM], [r_AK[i]], out=AK[i][:, :, 64:128], in0=v3(LX, 8),
                  in1=v3(EM, 8), op=ALU.mult)
            kb.op("pool", "tensor_tensor", [r_Rr, r_LX], [r_EX], out=EX, in0=Rr, in1=LX, op=ALU.mult)
            kb.op("dve", "tensor_scalar", [r_EX, r_rkc], [r_PRb], out=PRb, in0=EX, scalar1=rkc[:, i:i + 1],
                  scalar2=None, op0=ALU.mult)
            for j in range(4):
                kb.op("pe", "matmul", [r_PRb, r_hind], [r_ps[brk]], inc=(j == 3),
                      out=ps[brk][:, j * 8:(j + 1) * 8], lhsT=PRb[:, j * 128:(j + 1) * 128],
                      rhs=hind[:, i * 8:(i + 1) * 8], start=(i == 0 and j == 0), stop=(i == 3))

            for e in range(2):
                hp = slice(64 * e, 64 * e + 64)
                for half in range(2):
                    bg = gp.get()
                    for cc in range(4):
                        c = half * 4 + cc
                        kb.op("pe", "matmul", [r_AK[i], r_KR[i]], [r_ps[bg]], inc=(cc == 3),
                              out=ps[bg][:, cc * 128:(cc + 1) * 128], lhsT=AK[i][hp, c, :], rhs=KR[i][hp, c, :],
                              start=True, stop=True)
                    kb.op("dve", "tensor_tensor", [r_ps[bg], r_m12], [r_M12[i][e]],
                          out=M12[i][e][:, half * 4:half * 4 + 4, :], in0=v3(ps[bg], 4),
                          in1=m12.unsqueeze(1).to_broadcast([128, 4, 128]), op=ALU.mult)
            q, qt = 0, 0
            for e in range(2):
                hp = slice(64 * e, 64 * e + 64)
                bg = gp.get()
                for c in range(8):
                    kb.op("pe", "matmul", [r_AK[i], r_KR[i]], [r_ps[bg]], inc=(c == 7),
                          out=ps[bg][:, c * 64:(c + 1) * 64],
                          lhsT=KR[i][hp, c, :], rhs=AK[i][hp, c, 0:64], start=True, stop=True)
                kb.op("dve", "scalar_tensor_tensor", [r_ps[bg], r_ma], [r_QsS[i % 2][q]],
                      out=QsS[i % 2][q][0:64, :, e, :], in0=v3(ps[bg][0:64, :], 8), scalar=-1.0,
                      in1=ma[0:64, :].unsqueeze(1).to_broadcast([64, 8, 64]), op0=ALU.mult, op1=ALU.mult)
            for e in range(2):
                kb.op("dve", "tensor_scalar", [r_M12[i][e]], [r_QTsS[i % 2][qt]], out=QTsS[i % 2][qt][0:64, :, e, :],
                      in0=M12[i][e][0:64, :, 0:64], scalar1=-1.0, scalar2=None, op0=ALU.mult)
            kb.op("dve", "tensor_tensor", [r_QTsS[i % 2][qt], r_ident], [r_PTS[i % 2]],
                  out=PTS[i % 2][0:64].rearrange("p a b c -> p (a b) c"),
                  in0=QTsS[i % 2][qt][0:64].rearrange("p a b c -> p (a b) c"),
                  in1=i64.unsqueeze(1).to_broadcast([64, 16, 64]), op=ALU.add)
            stq[i] = [q, qt]

        def dbl(i, lvl):
            q, qt = stq[i]
            qn, qtn = 1 - q, 1 - qt
            for half in range(2):
                bg = gp.get()
                for cc in range(4):
                    c = half * 4 + cc
                    for e in range(2):
                        kb.op("pe", "matmul", [r_QsS[i % 2][q], r_QTsS[i % 2][qt]], [r_ps[bg]], inc=(cc == 3 and e == 1),
                              out=ps[bg][0:64, (cc * 2 + e) * 64:(cc * 2 + e + 1) * 64],
                              lhsT=QTsS[i % 2][qt][0:64, c, e, :], rhs=QsS[i % 2][q][0:64, c, e, :], start=True, stop=True)
                evac(QsS[i % 2][qn][0:64, half * 4:half * 4 + 4].rearrange("p a b c -> p (a b c)"), ps[bg][0:64, :],
                     [r_ps[bg]], [r_QsS[i % 2][qn]])
            if lvl < 5:
                for half in range(2):
                    bg = gp.get()
                    for cc in range(4):
                        c = half * 4 + cc
                        for e in range(2):
                            kb.op("pe", "matmul", [r_QsS[i % 2][q], r_QTsS[i % 2][qt]], [r_ps[bg]], inc=(cc == 3 and e == 1),
                                  out=ps[bg][0:64, (cc * 2 + e) * 64:(cc * 2 + e + 1) * 64],
                                  lhsT=QsS[i % 2][q][0:64, c, e, :], rhs=QTsS[i % 2][qt][0:64, c, e, :], start=True, stop=True)
                    evac(QTsS[i % 2][qtn][0:64, half * 4:half * 4 + 4].rearrange("p a b c -> p (a b c)"),
                         ps[bg][0:64, :], [r_ps[bg]], [r_QTsS[i % 2][qtn]])
            for half in range(2):
                bg = gp.get()
                for cc in range(4):
                    c = half * 4 + cc
                    for e in range(2):
                        kb.op("pe", "matmul", [r_QsS[i % 2][qn], r_PTS[i % 2]], [r_ps[bg]], inc=(cc == 3 and e == 1),
                              out=ps[bg][0:64, (cc * 2 + e) * 64:(cc * 2 + e + 1) * 64],
                              lhsT=QsS[i % 2][qn][0:64, c, e, :], rhs=PTS[i % 2][0:64, c, e, :], start=True, stop=True)
                pv = PTS[i % 2][0:64, half * 4:half * 4 + 4].rearrange("p a b c -> p (a b c)")
                kb.op("dve", "tensor_tensor", [r_ps[bg], r_PTS[i % 2]], [r_PTS[i % 2]], out=pv, in0=pv, in1=ps[bg][0:64, :],
                      op=ALU.add)
            q, qt = qn, qtn
            stq[i] = [q, qt]

        def post(i):
            q, qt = stq[i]
            bt = gp.get()
            for c in range(8):
                kb.op("pe", "transpose", [r_AK[i], r_ident], [r_ps[bt]], inc=(c == 7),
                      out=psb[bt][:, c * 128:(c + 1) * 128], in_=AK[i][:, c, :], identity=ident)
            evac(Kc[i], v3(psb[bt], 8), [r_ps[bt]], [r_Kc[i]])
            bt = gp.get()
            for c in range(8):
                kb.op("pe", "transpose", [r_KR[i], r_ident], [r_ps[bt]], inc=(c == 7),
                      out=psb[bt][0:64, c * 128:(c + 1) * 128], in_=KR[i][:, c, 0:64], identity=ident)
            evac(KTS[i % 2][0:64], v3(psb[bt][0:64, :], 8), [r_ps[bt]], [r_KTS[i % 2]])
            bt = gp.get()
            for c in range(8):
                kb.op("pe", "transpose", [r_VTp[i], r_ident], [r_ps[bt]], inc=(c == 7),
                      out=psb[bt][:, c * 128:(c + 1) * 128], in_=VTp[i][:, c * 64:c * 64 + 128], identity=ident)
            evac(X[64:128, :, i, :], v3(psb[bt][64:128, :], 8), [r_ps[bt]], [r_XV[i]])
            for half in range(2):
                bg = gp.get()
                for cc in range(4):
                    c = half * 4 + cc
                    kb.op("pe", "matmul", [r_KTS[i % 2], r_PTS[i % 2]], [r_ps[bg]], inc=(cc == 3),
                          out=ps[bg][:, cc * 128:(cc + 1) * 128], lhsT=KTS[i % 2][0:64, c, :],
                          rhs=PTS[i % 2][0:64, c].rearrange("p a b -> p (a b)"), start=True, stop=True)
                for e in range(2):
                    hp = slice(64 * e, 64 * e + 64)
                    evac(LW[i][hp, half * 4:half * 4 + 4, :], v3(ps[bg], 4)[hp, :, 64 * e:64 * e + 64],
                         [r_ps[bg]], [r_LW[i]], scale=-1.0)
            for half in range(2):
                bg = gp.get()
                for cc in range(4):
                    c = half * 4 + cc
                    for e in range(2):
                        kb.op("pe", "matmul", [r_M12[i][e], r_XV[i]], [r_ps[bg]], inc=(cc == 3 and e == 1),
                              out=ps[bg][:, (cc * 2 + e) * 64:(cc * 2 + e + 1) * 64],
                              lhsT=M12[i][e][64:128, c, :], rhs=X[64:128, c, i, 64 * e:64 * e + 64],
                              start=True, stop=True)
                evac(BVsS[i % 2][0:64, half * 4:half * 4 + 4].rearrange("p a b c -> p (a b c)"), ps[bg][0:64, :],
                     [r_ps[bg]], [r_BVsS[i % 2]])
            for half in range(2):
                bg = gp.get()
                for cc in range(4):
                    c = half * 4 + cc
                    for e in range(2):
                        kb.op("pe", "matmul", [r_PTS[i % 2], r_BVsS[i % 2]], [r_ps[bg]], inc=(cc == 3 and e == 1),
                              out=ps[bg][0:64, (cc * 2 + e) * 64:(cc * 2 + e + 1) * 64],
                              lhsT=PTS[i % 2][0:64, c, e, :], rhs=BVsS[i % 2][0:64, c, e, :], start=True, stop=True)
                evac(U0[0:64, half * 4:half * 4 + 4, i, :], v3(ps[bg][0:64, :], 4), [r_ps[bg]], [r_U0[i]], scale=-1.0)

        for g2 in range(2):
            for i in (2 * g2, 2 * g2 + 1):
                pre(i)
            for lvl in range(1, 6):
                for i in (2 * g2, 2 * g2 + 1):
                    dbl(i, lvl)
            for i in (2 * g2, 2 * g2 + 1):
                post(i)

        evac(rkS, ps[brk][:, 0:32], [r_ps[brk]], [r_rkS])
        for j in range(4):
            bo = BO[j % 2]
            r_bo = r_BO[j % 2]
            bt = gp.get()
            for i in range(4):
                kb.op("pe", "transpose", [r_VTp[i], r_ident], [r_ps[bt]], inc=(i == 3),
                      out=psb[bt][:, i * 128:(i + 1) * 128], in_=VTp[i][:, 64 + j * 128:64 + (j + 1) * 128],
                      identity=ident)
            kb.op("dve", "tensor_tensor", [r_ps[bt], r_rkS], [r_bo], out=v3(bo, 8), in0=v3(psb[bt][:, 0:512], 8),
                  in1=rkS[:, j * 8:(j + 1) * 8].unsqueeze(2).to_broadcast([128, 8, 64]), op=ALU.mult)
            kb.dma("pool", BD[t0 + j * 128:t0 + (j + 1) * 128, :], bo, [r_bo], [])

        gsrc = GAMALL[:, :, t * 8:t * 8 + 8] if d == 0 else GAMALL[:, :, t * 8 + 1:t * 8 + 9]
        kb.op("dve", "tensor_tensor", [r_GAM, r_cfl], [r_GP], out=GP, in0=gsrc,
              in1=cfl[:, t * 8:t * 8 + 8].unsqueeze(1).to_broadcast([128, 4, 8]), op=ALU.mult)
        for i in range(4):
            kb.op("dve", "tensor_tensor", [r_LW[i], r_GP], [r_LW[i]], out=LW[i], in0=LW[i],
                  in1=GP[:, i, :].unsqueeze(2).to_broadcast([128, 8, 64]), op=ALU.mult)
            kb.op("pool", "tensor_tensor", [r_KR[i], r_GP], [r_KR[i]], out=KR[i][:, :, 64:128],
                  in0=KR[i][:, :, 64:128], in1=GP[:, i, :].unsqueeze(2).to_broadcast([128, 8, 64]), op=ALU.mult)
        kb.op("dve", "tensor_tensor", [r_GP, r_ident], [r_DG], out=DG.rearrange("p a b c -> p (a b) c"),
              in0=ident.unsqueeze(1).to_broadcast([128, 32, 128]),
              in1=GP.rearrange("p a b -> p (a b)").unsqueeze(2).to_broadcast([128, 32, 128]), op=ALU.mult)

        chunks = list(range(8)) if d == 0 else list(range(7, -1, -1))
        for c in chunks:
            nxt = 1 - cur
            R0, r_R0 = Rbd[cur], r_Rbd[cur]
            R1, r_R1 = Rbd[nxt], r_Rbd[nxt]
            bu = 5
            for i in range(4):
                kb.op("pe", "matmul", [r_LW[i], r_R0], [r_ps[bu]], inc=(i == 3),
                      out=ps[bu][0:64, i * 128:(i + 1) * 128], lhsT=LW[i][:, c, :], rhs=R0[:, i, :],
                      start=True, stop=True)
            kb.op("dve", "tensor_tensor", [r_ps[bu]] + r_U0, [r_XU[c]], out=X[0:64, c, :, :],
                  in0=v3(ps[bu][0:64, :], 4), in1=U0[0:64, c, :, :], op=ALU.add)
            bs = 6
            for i in range(4):
                kb.op("pe", "matmul", [r_DG, r_R0], [r_ps[bs]], inc=False,
                      out=ps[bs][:, i * 128:(i + 1) * 128], lhsT=DG[:, i, c, :], rhs=R0[:, i, :],
                      start=True, stop=False)
                kb.op("pe", "matmul", [r_Kc[i], r_XU[c], r_XV[i]], [r_ps[bs]], inc=(i == 3),
                      out=ps[bs][:, i * 128:(i + 1) * 128], lhsT=Kc[i][:, c, :], rhs=X[:, c, i, :],
                      start=False, stop=True)
            kb.op("dve", "tensor_tensor", [r_ps[bs], r_bdm], [r_R1], out=R1.rearrange("p a b -> p (a b)"),
                  in0=ps[bs], in1=bdm, op=ALU.mult)
            by = 7
            for i in range(4):
                kb.op("pe", "matmul", [r_KR[i], r_R0], [r_ps[by]], inc=False,
                      out=ps[by][0:64, i * 128:(i + 1) * 128], lhsT=KR[i][:, c, 64:128], rhs=R0[:, i, :],
                      start=True, stop=False)
                for e in range(2):
                    kb.op("pe", "matmul", [r_M12[i][e], r_XU[c], r_XV[i]], [r_ps[by]], inc=(i == 3 and e == 1),
                          out=ps[by][0:64, i * 128 + 64 * e:i * 128 + 64 * e + 64], lhsT=M12[i][e][:, c, 64:128],
                          rhs=X[:, c, i, 64 * e:64 * e + 64], start=False, stop=(e == 1))
            yb = yi % 2
            yi += 1
            kb.op("act", "activation", [r_ps[by]], [r_YT[yb]], out=YT[yb][0:64, :], in_=ps[by][0:64, :], func=AF.Copy)
            ct0 = t0 + c * 64
            kb.dma("pool", YD[ct0:ct0 + 64, :], YT[yb][0:64, :], [r_YT[yb]], [])
            cur = nxt


GELU_C = 1.5957691216057308
GN_EPS = 64e-5


def gelu_tanh(kb, src, r_src, out, r_out, t1, r_t1, t2, r_t2):
    kb.op("act", "activation", [r_src], [r_t1], out=t1, in_=src, func=AF.Square)
    kb.op("dve", "tensor_scalar", [r_t1], [r_t1], out=t1, in0=t1, scalar1=0.044715, scalar2=1.0,
          op0=ALU.mult, op1=ALU.add)
    kb.op("dve", "tensor_tensor", [r_t1, r_src], [r_t1], out=t1, in0=t1, in1=src, op=ALU.mult)
    kb.op("act", "activation", [r_t1], [r_t2], out=t2, in_=t1, func=AF.Sigmoid, scale=GELU_C)
    kb.op("dve", "tensor_tensor", [r_t2, r_src], [r_out], out=out, in0=t2, in1=src, op=ALU.mult)


def bcload(kb, A, src, cols):
    t = A.f32(cols)
    r = Res()
    kb.dma("sp", t, src.partition_broadcast(128), [], [r])
    return t, r


def phase_mixc(kb, C, T, x_in, x_out, W, YD, BD):
    A = phase_begin(kb, C)
    ps, psb, r_ps = C.ps, C.ps_bf, C.r_ps
    ident, r_ident = C.ident, C.r_ident
    gt, r_gt = load_gcols(kb, A, W["mix_norm"][0], 8)
    Wp = v3(A.bf(8 * 1536), 8)
    r_Wp = rr(8)
    Wo = v3(A.bf(8 * 1024), 8)
    r_Wo = rr(8)
    alloc_wl(C, A)
    wab = W["ab_w_in"][0]
    for c in range(8):
        rows = wab[c * 128:(c + 1) * 128, :]
        load_weight(kb, C, Wp[:, c, 0:1024], r_Wp[c], rows[:, 0:1024], 1024, gt[:, c:c + 1], extra=[r_gt])
        load_weight(kb, C, Wp[:, c, 1024:1536], r_Wp[c], rows[:, 2560:3072], 512, gt[:, c:c + 1], extra=[r_gt])
        load_weight(kb, C, Wo[:, c, :], r_Wo[c], W["ab_w_out"][0][c * 128:(c + 1) * 128, :], 1024, 1.0)
    sgn, r_sgn = bcload(kb, A, W["sgu_norm"][0], 512)
    gnw, r_gnw = bcload(kb, A, W["rwkv_gn_w"][0], 512)
    gnb, r_gnb = bcload(kb, A, W["rwkv_gn_b"][0], 512)
    bcol = A.f32(4)
    r_bcol = Res()
    kb.dma("sp", bcol, W["sgu_b"][0].rearrange("g q -> q g"), [], [r_bcol], allow_slow_non_contiguous=True)
    wsT = v3(A.bf(512), 4)
    r_wsT = Res()
    for g in range(4):
        st, r_st = C.wl_st[g % 2], C.r_wl_st[g % 2]
        kb.dma("sp", st[:, 0:128], W["sgu_w"][0, g], [], [r_st])
        sb = A.bf(128)
        r_sb = Res()
        kb.op("dve", "tensor_copy", [r_st], [r_sb], out=sb, in_=st[:, 0:128])
        kb.op("pe", "transpose", [r_sb, r_ident], [r_ps[4]], out=psb[4][:, 0:128], in_=sb, identity=ident)
        kb.op("dve", "tensor_copy", [r_ps[4]], [r_wsT], out=wsT[:, g, :], in_=psb[4][:, 0:128])
    alloc_norm_tmps(C, A)
    xs = [A.f32(D) for _ in range(2)]
    r_xs = rr(2)
    xr = [A.f32(D) for _ in range(2)]
    r_xr = rr(2)
    xo = [A.f32(D) for _ in range(2)]
    r_xo = rr(2)
    hT = v3(A.bf(8 * 512), 8)
    r_hT = Res()
    tm = [A.f32(512) for _ in range(8)]
    r_tm = rr(8)
    ld = [[A.f32(512) for _ in range(4)] for _ in range(2)]
    r_ld = [rr(4) for _ in range(2)]
    vn = A.bf(512)
    r_vn = Res()
    yab = A.bf(1024)
    r_yab = Res()
    yT = v3(A.bf(1024), 8)
    r_yT = Res()
    st8 = [A.f32(8) for _ in range(3)]
    r_st8 = rr(3)
    ss1 = A.f32(1)
    rs1 = A.f32(1)
    r_s1 = Res()
    NT = T // 512
    xi = 0
    si = 0
    for t in range(NT):
        for j in range(4):
            r0 = t * 512 + j * 128
            bx = xi % 2
            xi += 1
            kb.dma("sp", xs[bx], x_in[r0:r0 + 128, :], [], [r_xs[bx]])
            norm_transpose(kb, C, xs[bx], r_xs[bx], hT, r_hT, j * 128)
        for j in range(4):
            r0 = t * 512 + j * 128
            k = si % 2
            si += 1
            L4, r_L4 = ld[k], r_ld[k]
            for n, src in enumerate((YD[0], YD[1], BD[0], BD[1])):
                kb.dma("sp", L4[n], src[r0:r0 + 128, :], [], [r_L4[n]])
            kb.dma("sp", xr[k], x_in[r0:r0 + 128, :], [], [r_xr[k]])
            for n in range(3):
                for c in range(8):
                    kb.op("pe", "matmul", [r_Wp[c], r_hT], [r_ps[n]], inc=(c == 7), out=ps[n],
                          lhsT=hT[:, c, j * 128:(j + 1) * 128], rhs=Wp[:, c, n * 512:(n + 1) * 512],
                          start=(c == 0), stop=(c == 7))
            gu, gv = tm[0], tm[1]
            gelu_tanh(kb, ps[0], r_ps[0], gu, r_tm[0], tm[2], r_tm[2], tm[3], r_tm[3])
            gelu_tanh(kb, ps[1], r_ps[1], gv, r_tm[1], tm[4], r_tm[4], tm[5], r_tm[5])
            kb.op("act", "activation", [r_tm[1]], [r_tm[4], r_s1], out=tm[4], in_=gv, func=AF.Square, accum_out=ss1)
            rstd_ops(kb, ss1, rs1, r_s1, r_s1, 512, EPS)
            kb.op("dve", "scalar_tensor_tensor", [r_tm[1], r_s1, r_sgn], [r_vn], out=vn, in0=gv, scalar=rs1,
                  in1=sgn, op0=ALU.mult, op1=ALU.mult)
            for g in range(4):
                kb.op("pe", "matmul", [r_wsT, r_vn], [r_ps[3]], inc=(g == 3), out=ps[3][:, g * 128:(g + 1) * 128],
                      lhsT=wsT[:, g, :], rhs=vn[:, g * 128:(g + 1) * 128], start=(g == 0), stop=(g == 3))
            for g in range(4):
                kb.op("dve", "scalar_tensor_tensor", [r_ps[3], r_bcol, r_tm[0]], [r_yab],
                      out=yab[:, g * 128:(g + 1) * 128], in0=ps[3][:, g * 128:(g + 1) * 128],
                      scalar=bcol[:, g:g + 1], in1=gu[:, g * 128:(g + 1) * 128], op0=ALU.add, op1=ALU.mult)
            y, yc, sq = tm[2], tm[3], tm[5]
            kb.op("pool", "tensor_tensor", [r_L4[0], r_L4[1]], [r_tm[2]], out=y, in0=L4[0], in1=L4[1], op=ALU.add)
            kb.op("dve", "tensor_reduce", [r_tm[2]], [r_st8[0]], out=st8[0], in_=v3(y, 8), axis=AX.X, op=ALU.add)
            kb.op("dve", "tensor_scalar", [r_st8[0]], [r_st8[0]], out=st8[0], in0=st8[0], scalar1=1.0 / 64,
                  scalar2=None, op0=ALU.mult)
            kb.op("dve", "tensor_tensor", [r_tm[2], r_st8[0]], [r_tm[3]], out=v3(yc, 8), in0=v3(y, 8),
                  in1=st8[0].unsqueeze(2).to_broadcast([128, 8, 64]), op=ALU.subtract)
            kb.op("act", "activation", [r_tm[3]], [r_tm[5]], out=sq, in_=yc, func=AF.Square)
            kb.op("dve", "tensor_reduce", [r_tm[5]], [r_st8[1]], out=st8[1], in_=v3(sq, 8), axis=AX.X, op=ALU.add)
            kb.op("act", "activation", [r_st8[1]], [r_st8[2]], out=st8[2], in_=st8[1], func=AF.Sqrt,
                  scale=1.0 / 64, bias=GN_EPS)
            kb.op("dve", "reciprocal", [r_st8[2]], [r_st8[2]], out=st8[2], in_=st8[2])
            kb.op("dve", "tensor_tensor", [r_tm[3], r_st8[2]], [r_tm[3]], out=v3(yc, 8), in0=v3(yc, 8),
                  in1=st8[2].unsqueeze(2).to_broadcast([128, 8, 64]), op=ALU.mult)
            kb.op("pool", "tensor_tensor", [r_tm[3], r_gnw], [r_tm[3]], out=yc, in0=yc, in1=gnw, op=ALU.mult)
            kb.op("pool", "tensor_tensor", [r_tm[3], r_gnb], [r_tm[3]], out=yc, in0=yc, in1=gnb, op=ALU.add)
            kb.op("pool", "tensor_tensor", [r_L4[2], r_L4[3]], [r_tm[5]], out=sq, in0=L4[2], in1=L4[3], op=ALU.add)
            kb.op("pool", "tensor_tensor", [r_tm[3], r_tm[5]], [r_tm[3]], out=yc, in0=yc, in1=sq, op=ALU.add)
            kb.op("act", "activation", [r_ps[2]], [r_tm[4]], out=tm[4], in_=ps[2], func=AF.Sigmoid)
            kb.op("dve", "tensor_tensor", [r_tm[3], r_tm[4]], [r_yab], out=yab[:, 512:1024], in0=yc, in1=tm[4],
                  op=ALU.mult)
            for c in range(8):
                kb.op("pe", "transpose", [r_yab, r_ident], [r_ps[4]], inc=(c == 7),
                      out=psb[4][:, c * 128:(c + 1) * 128], in_=yab[:, c * 128:(c + 1) * 128], identity=ident)
            kb.op("act", "activation", [r_ps[4]], [r_yT], out=yT, in_=v3(psb[4], 8), func=AF.Copy)
            for hf in range(2):
                bo = 5 if hf == 0 else 3
                for c in range(8):
                    kb.op("pe", "matmul", [r_yT, r_Wo[c]], [r_ps[bo]], inc=(c == 7), out=ps[bo],
                          lhsT=yT[:, c, :], rhs=Wo[:, c, hf * 512:(hf + 1) * 512], start=(c == 0), stop=(c == 7))
                kb.op("dve", "tensor_tensor", [r_ps[bo], r_xr[k]], [r_xo[k]], out=xo[k][:, hf * 512:(hf + 1) * 512],
                      in0=ps[bo], in1=xr[k][:, hf * 512:(hf + 1) * 512], op=ALU.add)
            kb.dma("pool", x_out[r0:r0 + 128, :], xo[k], [r_xo[k]], [])


PAD = 1024
NVT = 45
SLOPES = [2.0 ** (-8.0 * (h + 1) / 16.0) for h in range(16)]


def attn_consts(T, seg):
    c = {}
    p = np.arange(128)[:, None]
    for name, dil, nq in (("d1", 1, 128), ("d4", 4, 128), ("d16", 16, 32)):
        q = np.arange(nq)[None, :]
        for kb in range(2):
            ik = -64 + 128 * kb + p
            dist = np.abs(ik - q)
            c[f"{name}k{kb}"] = np.where(dist <= 64, -(dil * dist).astype(np.float32), np.float32(-1e30)).astype(np.float32)
    NT = T // 512
    vfl = np.zeros((128, NT, NVT), np.float32)
    pp = np.arange(128)
    for t in range(NT):
        t0 = t * 512
        s0 = (t0 // seg) * seg
        idx = 0

        def put(tok):
            nonlocal idx
            vfl[:, t, idx] = ((tok >= s0) & (tok < s0 + seg)).astype(np.float32)
            idx += 1
        for m in range(5):
            put(t0 - 64 + 128 * m + pp)
        for r in range(4):
            for kb in range(2):
                put(t0 + r + 4 * (-64 + 128 * kb + pp))
        for r in range(16):
            for kb in range(2):
                put(t0 + r + 16 * (-64 + 128 * kb + pp))
    c["vfl"] = vfl.reshape(128, NT * NVT)
    return c


ATT_SHAPES = {"d1k0": [128, 128], "d1k1": [128, 128], "d4k0": [128, 128], "d4k1": [128, 128],
              "d16k0": [128, 32], "d16k1": [128, 32]}


def phase_attn_qkv(kb, C, T, x_in, W, QT, KT, VD):
    A = phase_begin(kb, C)
    ps, psb, r_ps = C.ps, C.ps_bf, C.r_ps
    gt, r_gt = load_gcols(kb, A, W["mix_norm"][1], 8)
    Wp = v3(A.bf(8 * 3072), 8)
    r_Wp = rr(8)
    alloc_wl(C, A)
    for c in range(8):
        load_weight(kb, C, Wp[:, c, :], r_Wp[c], W["attn_w_in"][0][c * 128:(c + 1) * 128, :], 3072, gt[:, c:c + 1],
                    extra=[r_gt])
    bones, r_bones = cload(kb, A, C.cd["bones"], 128, bf=True)
    gq = A.f32(2)
    r_gq = Res()
    for e in range(2):
        kb.dma("sp", gq[64 * e:64 * e + 64, 0:1], W["attn_q_norm"][0].rearrange("(e o) -> e o", o=1), [], [r_gq])
        kb.dma("sp", gq[64 * e:64 * e + 64, 1:2], W["attn_k_norm"][0].rearrange("(e o) -> e o", o=1), [], [r_gq])
    kb.op("dve", "tensor_scalar", [r_gq], [r_gq], out=gq[:, 0:1], in0=gq[:, 0:1], scalar1=0.125, scalar2=None,
          op0=ALU.mult)
    zt = A.bf(1040)
    r_zt = Res()
    kb.op("pool", "memset", [], [r_zt], ap=zt, constant=0.0)
    for c in range(8):
        for side in range(2):
            c0 = 0 if side == 0 else PAD + T
            kb.dma("pool", KT[c * 128:(c + 1) * 128, c0:c0 + PAD], zt[:, 0:PAD], [r_zt], [])
            kb.dma("pool", VD[c0 + c * 128:c0 + (c + 1) * 128, :], zt, [r_zt], [])
    alloc_norm_tmps(C, A)
    xs = [A.f32(D) for _ in range(2)]
    r_xs = rr(2)
    hT = v3(A.bf(8 * 512), 8)
    r_hT = Res()
    sq = [A.bf(512) for _ in range(2)]
    r_sq = rr(2)
    rn = [A.f32(512) for _ in range(2)]
    r_rn = rr(2)
    qo = [A.bf(512) for _ in range(3)]
    r_qo = rr(3)
    va = [A.bf(1040) for _ in range(2)]
    r_va = rr(2)
    for b in range(2):
        kb.op("pool", "memset", [], [r_va[b]], ap=va[b], constant=1.0)
    NT = T // 512
    xi = qi = vi = 0
    gp = PsPool([0, 1, 2, 3, 4, 5])
    for t in range(NT):
        t0 = t * 512
        for j in range(4):
            r0 = t0 + j * 128
            bx = xi % 2
            xi += 1
            kb.dma("sp", xs[bx], x_in[r0:r0 + 128, :], [], [r_xs[bx]])
            norm_transpose(kb, C, xs[bx], r_xs[bx], hT, r_hT, j * 128)
        for ft in range(16):
            bm = gp.get()
            for c in range(8):
                kb.op("pe", "matmul", [r_Wp[c], r_hT], [r_ps[bm]], inc=(c == 7), out=ps[bm],
                      lhsT=Wp[:, c, ft * 128:(ft + 1) * 128], rhs=hT[:, c, :], start=(c == 0), stop=(c == 7))
            k = qi % 2
            k3 = qi % 3
            qi += 1
            kb.op("act", "activation", [r_ps[bm]], [r_sq[k]], out=sq[k], in_=ps[bm], func=AF.Square)
            bn = gp.get()
            kb.op("pe", "matmul", [r_bones, r_sq[k]], [r_ps[bn]], out=ps[bn], lhsT=bones, rhs=sq[k], start=True,
                  stop=True)
            kb.op("act", "activation", [r_ps[bn]], [r_rn[k]], out=rn[k], in_=ps[bn], func=AF.Sqrt, scale=1.0 / 64,
                  bias=EPS)
            kb.op("dve", "reciprocal", [r_rn[k]], [r_rn[k]], out=rn[k], in_=rn[k])
            isk = 1 if ft >= 8 else 0
            kb.op("dve", "scalar_tensor_tensor", [r_rn[k], r_ps[bm], r_gq], [r_qo[k3]], out=qo[k3], in0=ps[bm],
                  scalar=gq[:, isk:isk + 1], in1=rn[k], op0=ALU.mult, op1=ALU.mult)
            pr = ft % 8
            if isk:
                kb.dma("pool", KT[pr * 128:(pr + 1) * 128, PAD + t0:PAD + t0 + 512], qo[k3], [r_qo[k3]], [])
            else:
                kb.dma("pool", QT[pr * 128:(pr + 1) * 128, t0:t0 + 512], qo[k3], [r_qo[k3]], [])
        for j in range(4):
            k = vi % 2
            vi += 1
            for hf in range(2):
                bm = gp.get()
                for c in range(8):
                    kb.op("pe", "matmul", [r_Wp[c], r_hT], [r_ps[bm]], inc=(c == 7), out=ps[bm],
                          lhsT=hT[:, c, j * 128:(j + 1) * 128], rhs=Wp[:, c, 2048 + hf * 512:2048 + (hf + 1) * 512],
                          start=(c == 0), stop=(c == 7))
                dst = v3(va[k], 16)[:, hf * 8:(hf + 1) * 8, 0:64]
                if hf == 0:
                    kb.op("act", "activation", [r_ps[bm]], [r_va[k]], out=dst, in_=v3(ps[bm], 8), func=AF.Copy)
                else:
                    kb.op("dve", "tensor_copy", [r_ps[bm]], [r_va[k]], out=dst, in_=v3(ps[bm], 8))
            r0 = PAD + t0 + j * 128
            kb.dma("pool", VD[r0:r0 + 128, :], va[k], [r_va[k]], [])


def phase_attn(kb, C, T, x_in, x_out, W, QT, KT, VD):
    A = phase_begin(kb, C)
    ps, psb, r_ps = C.ps, C.ps_bf, C.r_ps
    Wo = v3(A.bf(16 * 1024), 16)
    r_Wo = rr(16)
    alloc_wl(C, A)
    for h in range(16):
        load_weight(kb, C, Wo[0:64, h, :], r_Wo[h], W["attn_w_out"][0][h * 64:(h + 1) * 64, :], 1024, 1.0, parts=64)
    dist = {}
    r_dist = {}
    for nm, shp in ATT_SHAPES.items():
        dist[nm], r_dist[nm] = cload(kb, A, C.cd[nm], shp[1])
    NT = T // 512
    vfl, r_vfl = cload(kb, A, C.cd["vfl"], NT * NVT)
    ones = A.f32(64)
    r_ones = Res()
    kb.op("pool", "memset", [], [r_ones], ap=ones, constant=1.0)
    Kw = [A.bf(2560) for _ in range(4)]
    r_Kw = rr(4)
    Qs = [A.bf(512) for _ in range(4)]
    r_Qs = rr(4)
    Vt = [A.bf(520) for _ in range(NVT)]
    r_Vt = rr(NVT)
    sc = [A.f32(512) for _ in range(4)]
    r_sc = rr(4)
    pT = [A.bf(512) for _ in range(4)]
    r_pT = rr(4)
    oT = v3(A.bf(16 * 512), 16)
    r_oT = rr(16)
    rden = [A.f32(512) for _ in range(2)]
    r_rden = rr(2)
    bcs = [A.f32(512) for _ in range(2)]
    r_bcs = rr(2)
    xr = [A.f32(D) for _ in range(2)]
    r_xr = rr(2)
    xo = [A.f32(D) for _ in range(2)]
    r_xo = rr(2)
    gp = PsPool([0, 1, 2, 3])
    si = pi = oi = 0
    for t in range(NT):
        t0 = t * 512
        for hg in range(2):
            for pl in range(4):
                pr = hg * 4 + pl
                kb.dma("sp", Kw[pl], KT[pr * 128:(pr + 1) * 128, t0:t0 + 2560], [], [r_Kw[pl]])
                kb.dma("sp", Qs[pl], QT[pr * 128:(pr + 1) * 128, t0:t0 + 512], [], [r_Qs[pl]])
            vspec = []
            for m in range(5):
                vspec.append((t0 + 960 + 128 * m, 1, 128))
            for r in range(4):
                for kbk in range(2):
                    vspec.append((t0 + 768 + r + 512 * kbk, 4, 128))
            for r in range(16):
                for kbk in range(2):
                    vspec.append((t0 + r + 2048 * kbk, 16, 128 if kbk == 0 else 32))
            for vi_, (row0, stp, n) in enumerate(vspec):
                src = VD[row0:row0 + stp * (n - 1) + 1:stp, hg * 520:(hg + 1) * 520]
                kb.dma("sp" if vi_ % 2 == 0 else "act", Vt[vi_][0:n, :], src, [], [r_Vt[vi_]])
                col = t * NVT + vi_
                if vi_ % 2 == 0:
                    kb.op("dve", "tensor_scalar", [r_Vt[vi_], r_vfl], [r_Vt[vi_]], out=Vt[vi_][0:n, :],
                          in0=Vt[vi_][0:n, :], scalar1=vfl[0:n, col:col + 1], scalar2=None, op0=ALU.mult)
                else:
                    kb.op("act", "activation", [r_Vt[vi_], r_vfl], [r_Vt[vi_]], out=Vt[vi_][0:n, :],
                          in_=Vt[vi_][0:n, :], func=AF.Copy, scale=vfl[0:n, col:col + 1])
            for hq in range(0, 8, 4):
                hls = [hq + u for u in range(4)]
                first = {hl: True for hl in hls}
                for name, dil, nq, nblk in (("d1", 1, 128, 4), ("d4", 4, 128, 4), ("d16", 16, 32, 16)):
                    for kbk in range(2):
                        for hl in hls:
                            h = hg * 8 + hl
                            pl, e = hl // 2, hl % 2
                            hp = slice(64 * e, 64 * e + 64)
                            po = 4 + hl % 4
                            nk = 32 if (dil == 16 and kbk == 1) else 128
                            bs = gp.get()
                            for b in range(nblk):
                                if dil == 1:
                                    k0 = 960 + 128 * (b + kbk)
                                    kcols = Kw[pl][hp, k0:k0 + nk]
                                    qcols = Qs[pl][hp, b * 128:(b + 1) * 128]
                                elif dil == 4:
                                    k0 = 768 + b + 512 * kbk
                                    kcols = Kw[pl][hp, k0:k0 + 4 * (nk - 1) + 1:4]
                                    qcols = Qs[pl][hp, b:b + 4 * 127 + 1:4]
                                else:
                                    k0 = b + 2048 * kbk
                                    kcols = Kw[pl][hp, k0:k0 + 16 * (nk - 1) + 1:16]
                                    qcols = Qs[pl][hp, b:b + 16 * 31 + 1:16]
                                kb.op("pe", "matmul", [r_Kw[pl], r_Qs[pl]], [r_ps[bs]], inc=(b == nblk - 1),
                                      out=ps[bs][0:nk, b * nq:(b + 1) * nq], lhsT=kcols, rhs=qcols, start=True,
                                      stop=True)
                            k2 = si % 4
                            si += 1
                            dn = f"{name}k{kbk}"
                            kb.op("dve", "scalar_tensor_tensor", [r_ps[bs], r_dist[dn]], [r_sc[k2]],
                                  out=v3(sc[k2][0:nk, :], nblk),
                                  in0=dist[dn][0:nk, :].unsqueeze(1).to_broadcast([nk, nblk, nq]), scalar=SLOPES[h],
                                  in1=v3(ps[bs][0:nk, :], nblk), op0=ALU.mult, op1=ALU.add)
                            k3 = pi % 4
                            pi += 1
                            kb.op("act", "activation", [r_sc[k2]], [r_pT[k3]], out=pT[k3][0:nk, :],
                                  in_=sc[k2][0:nk, :], func=AF.Exp)
                            for b in range(nblk):
                                if dil == 1:
                                    vidx = b + kbk
                                    ocols = ps[po][0:65, b * 128:(b + 1) * 128]
                                elif dil == 4:
                                    vidx = 5 + b * 2 + kbk
                                    ocols = ps[po][0:65, b:b + 4 * 127 + 1:4]
                                else:
                                    vidx = 13 + b * 2 + kbk
                                    ocols = ps[po][0:65, b:b + 16 * 31 + 1:16]
                                last = (dil == 16 and kbk == 1 and b == nblk - 1)
                                kb.op("pe", "matmul", [r_Vt[vidx], r_pT[k3]], [r_ps[po]], inc=(b == nblk - 1),
                                      out=ocols, lhsT=Vt[vidx][0:nk, hl * 65:(hl + 1) * 65],
                                      rhs=pT[k3][0:nk, b * nq:(b + 1) * nq], start=first[hl], stop=last)
                                first[hl] = False
                for hl in hls:
                    h = hg * 8 + hl
                    po = 4 + hl % 4
                    kd = hl % 2
                    kb.op("dve", "reciprocal", [r_ps[po]], [r_rden[kd]], out=rden[kd][64:65, :], in_=ps[po][64:65, :])
                    bb = gp.get()
                    kb.op("pe", "matmul", [r_ones, r_rden[kd]], [r_ps[bb]], out=ps[bb][0:64, :],
                          lhsT=ones[64:65, 0:64], rhs=rden[kd][64:65, :], start=True, stop=True)
                    kb.op("act", "activation", [r_ps[bb]], [r_bcs[kd]], out=bcs[kd][0:64, :], in_=ps[bb][0:64, :],
                          func=AF.Copy)
                    kb.op("dve", "tensor_tensor", [r_ps[po], r_bcs[kd]], [r_oT[h]], out=oT[0:64, h, :],
                          in0=ps[po][0:64, :], in1=bcs[kd][0:64, :], op=ALU.mult)
        for j in range(4):
            r0 = t0 + j * 128
            k = oi % 2
            oi += 1
            kb.dma("sp", xr[k], x_in[r0:r0 + 128, :], [], [r_xr[k]])
            for hf in range(2):
                bo = gp.get()
                for h in range(16):
                    kb.op("pe", "matmul", [r_oT[h], r_Wo[h]], [r_ps[bo]], inc=(h == 15), out=ps[bo],
                          lhsT=oT[0:64, h, j * 128:(j + 1) * 128], rhs=Wo[0:64, h, hf * 512:(hf + 1) * 512],
                          start=(h == 0), stop=(h == 15))
                kb.op("dve", "tensor_tensor", [r_ps[bo], r_xr[k]], [r_xo[k]], out=xo[k][:, hf * 512:(hf + 1) * 512],
                      in0=ps[bo], in1=xr[k][:, hf * 512:(hf + 1) * 512], op=ALU.add)
            kb.dma("pool", x_out[r0:r0 + 128, :], xo[k], [r_xo[k]], [])
```

```python
import numpy as np
from contextlib import ExitStack
import concourse.bass as bass
import concourse.mybir as mybir
from concourse.bass_utils import run_bass_kernel_spmd

F32 = mybir.dt.float32
BF16 = mybir.dt.bfloat16
AF = mybir.ActivationFunctionType
ALU = mybir.AluOpType
AX = mybir.AxisListType

D = 1024
FF = 2816
NFC = 22
NKC = 8
EPS = 1e-6
KD = 8


class Res:
    __slots__ = ("lw", "rs")

    def __init__(self):
        self.lw = None
        self.rs = {}


class KB:
    def __init__(self, nc, es):
        self.nc = nc
        self.engs = ("pe", "act", "dve", "pool", "sp")
        self.ops = {e: [] for e in self.engs}
        self.cnt = {e: 0 for e in self.engs}
        self.pending = {e: False for e in self.engs}
        self.waited = {e: {} for e in self.engs}
        self.semh = {}
        for e in ("pe", "act", "dve", "pool"):
            self.semh[e] = es.enter_context(nc.semaphore("s_" + e))
        self.dq = {}
        for q in ("sp", "pool", "act"):
            for j in range(KD):
                self.semh[("d", q, j)] = es.enter_context(nc.semaphore(f"d_{q}_{j}"))
            self.dq[q] = 0
        self.bar = {}

    def _deps(self, reads, writes):
        deps = dict(self.bar)

        def add(k, v):
            if deps.get(k, 0) < v:
                deps[k] = v

        for r in reads:
            if r.lw is not None:
                add(*r.lw)
        for w in writes:
            if w.lw is not None:
                add(*w.lw)
            for k, v in w.rs.items():
                add(k, v)
        return deps

    def _mark(self, me, reads, writes):
        k, v = me
        for r in reads:
            if r.rs.get(k, 0) < v:
                r.rs[k] = v
        for w in writes:
            w.lw = me
            w.rs = {}

    def _waits(self, eng, deps):
        waits = []
        wd = self.waited[eng]
        for k, v in deps.items():
            if k == "pe" and eng == "pe":
                continue
            if wd.get(k, 0) >= v:
                continue
            wd[k] = v
            waits.append((k, v))
        return waits

    def op(self, eng, name, reads=(), writes=(), inc=True, **kw):
        fn = (lambda e: getattr(e, name)(**kw))
        deps = self._deps(reads, writes)
        waits = self._waits(eng, deps)
        if inc:
            self.cnt[eng] += 1
            me = (eng, self.cnt[eng])
            self.pending[eng] = False
            self.ops[eng].append((fn, waits, (eng, 1)))
        else:
            me = (eng, self.cnt[eng] + 1)
            self.pending[eng] = True
            self.ops[eng].append((fn, waits, None))
        self._mark(me, reads, writes)

    def dma(self, q, out, in_, reads=(), writes=(), **kw):
        m = self.dq[q]
        self.dq[q] += 1
        j = m % KD
        val = 16 * (m // KD + 1)
        sk = ("d", q, j)
        deps = self._deps(reads, writes)
        if m >= KD and deps.get(sk, 0) < val - 16:
            deps[sk] = val - 16
        waits = self._waits(q, deps)
        self.ops[q].append((lambda e: e.dma_start(out=out, in_=in_, **kw), waits, (sk, 16)))
        self._mark((sk, val), reads, writes)

    def barrier(self):
        for e in self.engs:
            assert not self.pending[e], e
        b = {}
        for e in ("pe", "act", "dve", "pool"):
            if self.cnt[e]:
                b[e] = self.cnt[e]
        for q, n in self.dq.items():
            for j in range(KD):
                c = (n - j + KD - 1) // KD if n > j else 0
                if c:
                    b[("d", q, j)] = 16 * c
        self.bar = b

    def finish(self):
        self.barrier()
        for e in self.engs:
            waits = self._waits(e, dict(self.bar))
            if waits:
                self.ops[e].append((None, waits, None))

    def emit(self):
        nc = self.nc
        with nc.Block() as block:
            def mk(e):
                def body(eng):
                    for fn, waits, inc in self.ops[e]:
                        for k, v in waits:
                            eng.wait_ge(self.semh[k], v)
                        if fn is None:
                            continue
                        ins = fn(eng)
                        if inc is not None:
                            ins.then_inc(self.semh[inc[0]], inc[1])
                return body
            block.tensor(mk("pe"))
            block.scalar(mk("act"))
            block.vector(mk("dve"))
            block.gpsimd(mk("pool"))
            block.sync(mk("sp"))


class Arena:
    def __init__(self, ap, ncols):
        self.ap = ap
        self.n = ncols
        self.off = 0

    def reset(self):
        self.off = 0

    def f32(self, cols):
        a = self.ap[:, self.off:self.off + cols]
        self.off += cols
        assert self.off <= self.n, ("sbuf arena overflow", self.off, self.n)
        return a

    def bf(self, cols):
        c32 = (cols + 1) // 2
        return self.f32(c32).bitcast(BF16)[:, 0:cols]


def rr(n):
    return [Res() for _ in range(n)]


class Ctx:
    pass


def rstd_ops(kb, ss, rstd, r_ss, r_rstd, n, eps):
    kb.op("act", "activation", [r_ss], [r_rstd], out=rstd, in_=ss, func=AF.Sqrt, scale=1.0 / n, bias=eps)
    kb.op("dve", "reciprocal", [r_rstd], [r_rstd], out=rstd, in_=rstd)


def norm_transpose(kb, C, xs, r_xs, hT, r_hT, col0):
    i = C.nt_i
    C.nt_i += 1
    sq, ss, rstd, hb = C.nt_sq, C.nt_ss[i % 2], C.nt_rstd[i % 2], C.nt_hb[i % 2]
    r_sq, r_ss, r_rstd, r_hb = C.r_nt_sq, C.r_nt_ss[i % 2], C.r_nt_rstd[i % 2], C.r_nt_hb[i % 2]
    kb.op("act", "activation", [r_xs], [r_sq, r_ss], out=sq, in_=xs, func=AF.Square, accum_out=ss)
    rstd_ops(kb, ss, rstd, r_ss, r_rstd, D, EPS)
    if i % 2 == 0:
        kb.op("act", "activation", [r_xs, r_rstd], [r_hb], out=hb, in_=xs, func=AF.Copy, scale=rstd)
    else:
        kb.op("dve", "tensor_scalar", [r_xs, r_rstd], [r_hb], out=hb, in0=xs, scalar1=rstd, scalar2=None,
              op0=ALU.mult)
    pt = C.ps_bf[6 + i % 2].rearrange("p (c m) -> p c m", c=8)
    r_pt = C.r_ps[6 + i % 2]
    for c in range(8):
        kb.op("pe", "transpose", [r_hb, C.r_ident], [r_pt], inc=(c == 7),
              out=pt[:, c, :], in_=hb[:, c * 128:(c + 1) * 128], identity=C.ident)
    if i % 2 == 0:
        kb.op("dve", "tensor_copy", [r_pt], [r_hT], out=hT[:, :, col0:col0 + 128], in_=pt)
    else:
        kb.op("act", "activation", [r_pt], [r_hT], out=hT[:, :, col0:col0 + 128], in_=pt, func=AF.Copy)
    return rstd, r_rstd


def alloc_norm_tmps(C, A):
    C.nt_i = 0
    C.nt_sq = A.bf(1024)
    C.r_nt_sq = Res()
    C.nt_ss = [A.f32(1), A.f32(1)]
    C.r_nt_ss = rr(2)
    C.nt_rstd = [A.f32(1), A.f32(1)]
    C.r_nt_rstd = rr(2)
    hb0 = A.bf(1024)
    C.nt_hb = [hb0, hb0]
    r0 = Res()
    C.r_nt_hb = [r0, r0]


def load_weight(kb, C, dst, r_dst, src_rows, ncols, scale, extra=(), piece=704, parts=128):
    for c0 in range(0, ncols, piece):
        n = min(piece, ncols - c0)
        i = C.wl_i
        C.wl_i += 1
        st, r_st = C.wl_st[i % 2], C.r_wl_st[i % 2]
        kb.dma("sp", st[0:parts, 0:n], src_rows[:, c0:c0 + n], [], [r_st])
        eng = ("act", "dve")[i % 2]
        if eng == "act":
            kb.op("act", "activation", [r_st] + list(extra), [r_dst], out=dst[:, c0:c0 + n], in_=st[0:parts, 0:n],
                  func=AF.Copy, scale=scale)
        else:
            kb.op(eng, "tensor_scalar", [r_st] + list(extra), [r_dst], out=dst[:, c0:c0 + n], in0=st[0:parts, 0:n],
                  scalar1=scale, scalar2=None, op0=ALU.mult)


def alloc_wl(C, A, piece=704):
    C.wl_i = 0
    C.wl_st = [A.f32(piece), A.f32(piece)]
    C.r_wl_st = rr(2)


def load_gcols(kb, A, g_d, n):
    gt = A.f32(n)
    r = Res()
    kb.dma("sp", gt, g_d.rearrange("(c p) -> p c", p=128), [], [r], allow_slow_non_contiguous=True)
    return gt, r


def phase_begin(kb, C):
    A = C.A
    kb.barrier()
    A.reset()
    C.ident = A.bf(128)
    C.r_ident = Res()
    idf = A.f32(128)
    r_idf = Res()
    kb.dma("sp", idf, C.ident_d, [], [r_idf])
    kb.op("dve", "tensor_copy", [r_idf], [C.r_ident], out=C.ident, in_=idf)
    return A


def phase_ffn(kb, C, T, x_in, x_out, w_in, w_out, g_d, gfin_d):
    A = phase_begin(kb, C)
    W1 = A.bf(8 * 2 * FF).rearrange("p (c n) -> p c n", c=8)
    r_W1 = rr(8)
    W2 = A.bf(NFC * D).rearrange("p (c n) -> p c n", c=NFC)
    r_W2 = rr(NFC)
    gt, r_gt = load_gcols(kb, A, g_d, 8)
    alloc_wl(C, A)
    for c in range(8):
        load_weight(kb, C, W1[:, c, :], r_W1[c], w_in[c * 128:(c + 1) * 128, :], 2 * FF, gt[:, c:c + 1],
                    extra=[r_gt])
    for f in range(NFC):
        load_weight(kb, C, W2[:, f, :], r_W2[f], w_out[f * 128:(f + 1) * 128, :], D, 0.5)
    gfin = None
    r_gfin = Res()
    if gfin_d is not None:
        gfin = A.f32(D)
        kb.dma("sp", gfin, gfin_d.partition_broadcast(128), [], [r_gfin])
    alloc_norm_tmps(C, A)
    NXS = 2
    xs = [A.f32(D) for _ in range(NXS)]
    r_xs = rr(NXS)
    xr = [A.f32(D) for _ in range(2)]
    r_xr = rr(2)
    xo = [A.f32(D) for _ in range(2)]
    r_xo = rr(2)
    hT0 = A.bf(8 * 512).rearrange("p (c n) -> p c n", c=8)
    hT = [hT0, hT0]
    r_hT0 = Res()
    r_hT = [r_hT0, r_hT0]
    aT = A.bf(NFC * 512).rearrange("p (c n) -> p c n", c=NFC)
    r_aT = rr(NFC)
    sg = [A.bf(512) for _ in range(2)]
    r_sg = rr(2)
    fs_ss = [A.f32(1) for _ in range(2)]
    r_fs_ss = rr(2)
    fs_rstd = [A.f32(1) for _ in range(2)]
    r_fs_rstd = rr(2)
    fs_sq = C.nt_sq
    r_fs_sq = C.r_nt_sq
    ps, r_ps = C.ps, C.r_ps
    ntile = T // 512
    xi = 0
    gi = 0
    oi = 0
    for t in range(ntile):
        h, r_h = hT[t % 2], r_hT[t % 2]
        for j in range(4):
            r0 = t * 512 + j * 128
            b = xi % NXS
            xi += 1
            kb.dma("sp", xs[b], x_in[r0:r0 + 128, :], [], [r_xs[b]])
            norm_transpose(kb, C, xs[b], r_xs[b], h, r_h, j * 128)
        for f in range(NFC):
            pg, pu = 2 * (gi % 2), 2 * (gi % 2) + 1
            gi += 1
            for c in range(8):
                kb.op("pe", "matmul", [r_W1[c], r_h], [r_ps[pg]], inc=(c == 7),
                      out=ps[pg], lhsT=W1[:, c, f * 128:(f + 1) * 128], rhs=h[:, c, :],
                      start=(c == 0), stop=(c == 7))
            for c in range(8):
                kb.op("pe", "matmul", [r_W1[c], r_h], [r_ps[pu]], inc=(c == 7),
                      out=ps[pu], lhsT=W1[:, c, FF + f * 128:FF + (f + 1) * 128], rhs=h[:, c, :],
                      start=(c == 0), stop=(c == 7))
            s, r_s = sg[f % 2], r_sg[f % 2]
            kb.op("act", "activation", [r_ps[pg]], [r_s], out=s, in_=ps[pg], func=AF.Silu)
            kb.op("dve", "tensor_tensor", [r_s, r_ps[pu]], [r_aT[f]], out=aT[:, f, :], in0=s, in1=ps[pu],
                  op=ALU.mult)
        for j in range(4):
            r0 = t * 512 + j * 128
            b = oi % 2
            oi += 1
            kb.dma("sp", xr[b], x_in[r0:r0 + 128, :], [], [r_xr[b]])
            for hf in range(2):
                py = 4 + hf
                for f in range(NFC):
                    kb.op("pe", "matmul", [r_aT[f], r_W2[f]], [r_ps[py]], inc=(f == NFC - 1),
                          out=ps[py], lhsT=aT[:, f, j * 128:(j + 1) * 128], rhs=W2[:, f, hf * 512:(hf + 1) * 512],
                          start=(f == 0), stop=(f == NFC - 1))
                kb.op("dve", "tensor_tensor", [r_ps[py], r_xr[b]], [r_xo[b]],
                      out=xo[b][:, hf * 512:(hf + 1) * 512], in0=ps[py], in1=xr[b][:, hf * 512:(hf + 1) * 512],
                      op=ALU.add)
            if gfin_d is not None:
                ss, rstd = fs_ss[b], fs_rstd[b]
                kb.op("act", "activation", [r_xo[b]], [r_fs_sq, r_fs_ss[b]], out=fs_sq, in_=xo[b],
                      func=AF.Square, accum_out=ss)
                rstd_ops(kb, ss, rstd, r_fs_ss[b], r_fs_rstd[b], D, EPS)
                kb.op("dve", "scalar_tensor_tensor", [r_xo[b], r_fs_rstd[b], r_gfin], [r_xo[b]],
                      out=xo[b], in0=xo[b], scalar=rstd, in1=gfin, op0=ALU.mult, op1=ALU.mult)
            kb.dma("pool", x_out[r0:r0 + 128, :], xo[b], [r_xo[b]], [])


CONST_SHAPES = {"bones": [128, 128], "hind": [128, 32], "m12f": [128, 128], "m12b": [128, 128],
                "maf": [64, 64], "mab": [64, 64], "cmask": [128, 512], "bdmask": [128, 512]}


def make_consts(T, seg):
    c = {}
    p = np.arange(128)
    c["bones"] = (p[:, None] // 64 == p[None, :] // 64).astype(np.float32)
    hind = np.zeros((128, 4, 8), np.float32)
    for i in range(4):
        for e in range(2):
            hind[64 * e:64 * e + 64, i, 2 * i + e] = 1.0
    c["hind"] = hind.reshape(128, 32)
    j = p[:, None] % 64
    tt = p[None, :]
    c["m12f"] = np.where(tt < 64, j < tt, j <= tt - 64).astype(np.float32)
    c["m12b"] = np.where(tt < 64, j > tt, j >= tt - 64).astype(np.float32)
    q = np.arange(64)
    c["maf"] = (q[None, :] < q[:, None]).astype(np.float32)
    c["mab"] = (q[None, :] > q[:, None]).astype(np.float32)
    cm = np.ones((128, 512), np.float32)
    cm[:, ::64] = 0.0
    c["cmask"] = cm
    bd = np.zeros((128, 4, 128), np.float32)
    bd[0:64, :, 0:64] = 1.0
    bd[64:128, :, 64:128] = 1.0
    c["bdmask"] = bd.reshape(128, 512)
    NT = T // 512
    NCH = T // 64
    tf = np.ones((2, NT), np.float32)
    for t in range(NT):
        if (t * 512) % seg == 0:
            tf[0, t] = 0.0
        if (t * 512 + 512) % seg == 0:
            tf[1, t] = 0.0
    c["tflags"] = tf
    cf = np.ones((NCH,), np.float32)
    cb = np.ones((NCH,), np.float32)
    for ch in range(NCH):
        if (ch * 64) % seg == 0:
            cf[ch] = 0.0
        if (ch * 64 + 64) % seg == 0:
            cb[ch] = 0.0
    c["cflagf"] = cf
    c["cflagb"] = cb
    return c


def g_d_bc(g_d):
    return g_d.partition_broadcast(128)


def build(T, stage=99):
    nc = bass.Bass("TRN2", target_bir_lowering=False)
    dt = lambda name, shape, kind="ExternalInput": nc.dram_tensor(name, list(shape), F32, kind=kind).ap()
    x = dt("x", [T, D])
    y = dt("y", [T, D], "ExternalOutput")
    ident_d = dt("ident", [128, 128])
    wd = {}
    for nm, shp in (("ffn1_norm", [2, D]), ("ffn1_w_in", [2, D, 2 * FF]), ("ffn1_w_out", [2, FF, D]),
                    ("mix_norm", [2, D]), ("ffn2_norm", [2, D]), ("ffn2_w_in", [2, D, 2 * FF]),
                    ("ffn2_w_out", [2, FF, D]), ("block_norm", [2, D])):
        wd[nm] = dt(nm, shp)
    for nm, shp in (("ab_w_in", [1, D, DAB]), ("ab_w_out", [1, D, D]), ("sgu_norm", [1, 512]),
                    ("sgu_w", [1, 4, 128, 128]), ("sgu_b", [1, 4, 128]), ("rwkv_mu_rkv", [1, 2, 1536]),
                    ("rwkv_mu_wa", [1, 2, 128]), ("rwkv_w0", [1, 2, 512]), ("rwkv_w_up", [1, 2, 64, 512]),
                    ("rwkv_a0", [1, 2, 512]), ("rwkv_a_up", [1, 2, 64, 512]), ("rwkv_k_k", [1, 512]),
                    ("rwkv_k_a", [1, 512]), ("rwkv_r_k", [1, 8, 64]), ("rwkv_gn_w", [1, 512]),
                    ("rwkv_gn_b", [1, 512]), ("attn_w_in", [1, D, 3 * D]), ("attn_w_out", [1, D, D]),
                    ("attn_q_norm", [1, 64]), ("attn_k_norm", [1, 64])):
        wd[nm] = dt(nm, shp)
    cd = {}
    for nm, shp in CONST_SHAPES.items():
        cd[nm] = dt(nm, shp)
    cd["tflags"] = dt("tflags", [2, T // 512])
    cd["cflagf"] = dt("cflagf", [T // 64])
    cd["cflagb"] = dt("cflagb", [T // 64])
    xa = dt("xa", [T, D], "Internal")
    xb = dt("xb", [T, D], "Internal")
    for nm, shp in ATT_SHAPES.items():
        cd[nm] = dt(nm, shp)
    cd["vfl"] = dt("vfl", [128, (T // 512) * NVT])
    QT = nc.dram_tensor("qt_s", [D, T], BF16, kind="Internal").ap()
    KT = nc.dram_tensor("kt_s", [D, T + 2 * PAD], BF16, kind="Internal").ap()
    VD = nc.dram_tensor("vd_s", [T + 2 * PAD, 1040], BF16, kind="Internal").ap()
    dbg = stage in (3, 4)
    YD = [dt(f"yd{i}", [T, 512], "ExternalOutput" if dbg else "Internal") for i in range(2)]
    BD = [dt(f"bd{i}", [T, 512], "ExternalOutput" if dbg else "Internal") for i in range(2)]
    with ExitStack() as es:
        arena = es.enter_context(nc.sbuf_tensor("arena", [128, 53200], F32))
        C = Ctx()
        C.A = Arena(arena, 53200)
        C.ident_d = ident_d
        C.cd = cd
        C.ps = []
        C.ps_bf = []
        C.r_ps = rr(8)
        for i in range(8):
            p = es.enter_context(nc.psum_tensor(f"ps{i}", [128, 512], F32))
            C.ps.append(p[:, :])
            C.ps_bf.append(p[:, :].bitcast(BF16))
        kb = KB(nc, es)
        if stage == 1:
            phase_ffn(kb, C, T, x, y, wd["ffn1_w_in"][0], wd["ffn1_w_out"][0], wd["ffn1_norm"][0], None)
        elif stage == 3:
            phase_ffn(kb, C, T, x, xa, wd["ffn1_w_in"][0], wd["ffn1_w_out"][0], wd["ffn1_norm"][0], None)
            phase_rwkv(kb, C, T, 0, xa, wd, YD[0], BD[0])
            phase_rwkv(kb, C, T, 1, xa, wd, YD[1], BD[1])
        elif stage == 4:
            phase_ffn(kb, C, T, x, xa, wd["ffn1_w_in"][0], wd["ffn1_w_out"][0], wd["ffn1_norm"][0], None)
            phase_rwkv(kb, C, T, 0, xa, wd, YD[0], BD[0])
            phase_rwkv(kb, C, T, 1, xa, wd, YD[1], BD[1])
            phase_mixc(kb, C, T, xa, y, wd, YD, BD)
        elif stage == 2:
            phase_ffn(kb, C, T, x, xa, wd["ffn1_w_in"][0], wd["ffn1_w_out"][0], wd["ffn1_norm"][0], None)
            phase_ffn(kb, C, T, xa, y, wd["ffn2_w_in"][0], wd["ffn2_w_out"][0], wd["ffn2_norm"][0],
                      wd["block_norm"][0])
        else:
            phase_ffn(kb, C, T, x, xa, wd["ffn1_w_in"][0], wd["ffn1_w_out"][0], wd["ffn1_norm"][0], None)
            phase_rwkv(kb, C, T, 0, xa, wd, YD[0], BD[0])
            phase_rwkv(kb, C, T, 1, xa, wd, YD[1], BD[1])
            phase_mixc(kb, C, T, xa, xb, wd, YD, BD)
            phase_ffn(kb, C, T, xb, xa, wd["ffn2_w_in"][0], wd["ffn2_w_out"][0], wd["ffn2_norm"][0],
                      wd["block_norm"][0] if True else None)
            if stage == 5:
                pass
            phase_ffn(kb, C, T, xa, xb, wd["ffn1_w_in"][1], wd["ffn1_w_out"][1], wd["ffn1_norm"][1], None)
            phase_attn_qkv(kb, C, T, xb, wd, QT, KT, VD)
            phase_attn(kb, C, T, xb, xa if stage != 7 else y, wd, QT, KT, VD)
            if stage != 7:
                phase_ffn(kb, C, T, xa, y, wd["ffn2_w_in"][1], wd["ffn2_w_out"][1], wd["ffn2_norm"][1],
                          wd["block_norm"][1])
        kb.finish()
        kb.emit()
    return nc


WNAMES = ("ffn1_norm", "ffn1_w_in", "ffn1_w_out", "mix_norm", "ffn2_norm", "ffn2_w_in", "ffn2_w_out",
          "block_norm", "ab_w_in", "ab_w_out", "sgu_norm", "sgu_w", "sgu_b", "rwkv_mu_rkv", "rwkv_mu_wa",
          "rwkv_w0", "rwkv_w_up", "rwkv_a0", "rwkv_a_up", "rwkv_k_k", "rwkv_k_a", "rwkv_r_k", "rwkv_gn_w",
          "rwkv_gn_b", "attn_w_in", "attn_w_out", "attn_q_norm", "attn_k_norm")


def run_streams(streams, weights, stage=99, segs=None, raw=False):
    T = streams[0].shape[0]
    nc = build(T, stage)
    ident = np.eye(128, dtype=np.float32)
    in_maps = []
    for si, s in enumerate(streams):
        m = {"x": np.ascontiguousarray(s), "ident": ident}
        for k in WNAMES:
            m[k] = weights[k]
        m.update(make_consts(T, segs[si] if segs is not None else T))
        m.update(attn_consts(T, segs[si] if segs is not None else T))
        in_maps.append(m)
    res = run_bass_kernel_spmd(nc, in_maps, core_ids=list(range(len(streams))))
    if raw:
        return res.results
    return [r["y"] for r in res.results]


def kernel(**inputs):
    xp = np.asarray(inputs["x_prompt"], dtype=np.float32)
    xsm = np.asarray(inputs["x_sample"], dtype=np.float32)
    weights = {k: np.ascontiguousarray(np.asarray(v, dtype=np.float32)) for k, v in inputs.items()
               if k not in ("x_prompt", "x_sample")}
    B, S, _ = xp.shape
    TT = xsm.shape[1]
    per = TT // S
    streams = [xsm[0]]
    for i in range(B // per):
        streams.append(xp[i * per:(i + 1) * per].reshape(TT, D))
    while len(streams) < 8:
        streams.append(streams[1])
    segs = [TT] + [S] * (len(streams) - 1)
    outs = run_streams(streams, weights, segs=segs)
    y_sample = outs[0].reshape(1, TT, D)
    y_prompt = np.concatenate([outs[1 + i].reshape(per, S, D) for i in range(B // per)], axis=0)
    return (y_prompt, y_sample)


DAB = 3328
import os
STOP = int(os.environ.get('KSTOP', '0'))
LOGC = -0.6065306597126334


def v3(ap, a):
    return ap.rearrange("p (a b) -> p a b", a=a)


def v4(ap, a, b):
    return ap.rearrange("p (a b c) -> p a b c", a=a, b=b)


def cload(kb, A, src, cols, bf=False, parts=128):
    t32 = A.f32(cols)
    r = Res()
    kb.dma("sp", t32[0:parts, :], src, [], [r])
    if not bf:
        return t32, r
    tb = A.bf(cols)
    rb = Res()
    kb.op("dve", "tensor_copy", [r], [rb], out=tb[0:parts, :], in_=t32[0:parts, :])
    return tb, rb


class PsPool:
    def __init__(self, banks):
        self.banks = banks
        self.i = 0

    def get(self):
        b = self.banks[self.i % len(self.banks)]
        self.i += 1
        return b


def phase_rwkv(kb, C, T, d, x_in, W, YD, BD):
    A = phase_begin(kb, C)
    ps, psb, r_ps = C.ps, C.ps_bf, C.r_ps
    ident, r_ident = C.ident, C.r_ident
    NT = T // 512
    NCH = T // 64
    gp = PsPool([0, 1, 2, 3])
    ev = [0]

    def evac(out, in_, reads, writes, scale=None):
        ev[0] += 1
        if ev[0] % 2 == 0:
            if scale is None:
                kb.op("act", "activation", reads, writes, out=out, in_=in_, func=AF.Copy)
            else:
                kb.op("act", "activation", reads, writes, out=out, in_=in_, func=AF.Copy, scale=scale)
        else:
            if scale is None:
                kb.op("dve", "tensor_copy", reads, writes, out=out, in_=in_)
            else:
                kb.op("dve", "tensor_scalar", reads, writes, out=out, in0=in_, scalar1=scale, scalar2=None,
                      op0=ALU.mult)

    bones, r_bones = cload(kb, A, C.cd["bones"], 128, bf=True)
    hind, r_hind = cload(kb, A, C.cd["hind"], 32, bf=True)
    m12, r_m12 = cload(kb, A, C.cd["m12f" if d == 0 else "m12b"], 128, bf=True)
    ma, r_ma = cload(kb, A, C.cd["maf" if d == 0 else "mab"], 64, bf=True, parts=64)
    cmask, r_cmask = cload(kb, A, C.cd["cmask"], 512)
    bdm, r_bdm = cload(kb, A, C.cd["bdmask"], 512, bf=True)
    i64 = ident[0:64, 0:64]
    tfl = A.f32(NT)
    r_tfl = Res()
    kb.dma("sp", tfl[0:2, :], C.cd["tflags"], [], [r_tfl])
    cfl = A.f32(NCH)
    r_cfl = Res()
    kb.dma("sp", cfl, C.cd["cflagf" if d == 0 else "cflagb"].partition_broadcast(128), [], [r_cfl])
    gt, r_gt = load_gcols(kb, A, W["mix_norm"][0], 8)
    mu_r, r_mu_r = load_gcols(kb, A, W["rwkv_mu_rkv"][0, d], 12)
    mu_w, r_mu_w = load_gcols(kb, A, W["rwkv_mu_wa"][0, d], 1)
    w0c, r_w0c = load_gcols(kb, A, W["rwkv_w0"][0, d], 4)
    a0c, r_a0c = load_gcols(kb, A, W["rwkv_a0"][0, d], 4)
    kkc, r_kkc = load_gcols(kb, A, W["rwkv_k_k"][0], 4)
    kac, r_kac = load_gcols(kb, A, W["rwkv_k_a"][0], 4)
    rkc, r_rkc = load_gcols(kb, A, W["rwkv_r_k"][0].rearrange("h n -> (h n)"), 4)
    r_cc = [r_w0c, r_a0c, r_kkc, r_kac, r_rkc, r_mu_r, r_mu_w]
    NW = 1664
    Wp = v3(A.bf(8 * NW), 8)
    r_Wp = rr(8)
    alloc_wl(C, A)
    wab = W["ab_w_in"][0]
    wa0 = 3072 + d * 128
    for c in range(8):
        rows = wab[c * 128:(c + 1) * 128, :]
        load_weight(kb, C, Wp[:, c, 0:1536], r_Wp[c], rows[:, 1024:2560], 1536, gt[:, c:c + 1], extra=[r_gt])
        load_weight(kb, C, Wp[:, c, 1536:NW], r_Wp[c], rows[:, wa0:wa0 + 128], 128, gt[:, c:c + 1], extra=[r_gt])
    WUP = A.bf(512)
    r_WUP = Res()
    AUP = A.bf(512)
    r_AUP = Res()
    st, r_st = C.wl_st[0], C.r_wl_st[0]
    kb.dma("sp", st[0:64, 0:512], W["rwkv_w_up"][0, d], [], [r_st])
    kb.op("dve", "tensor_copy", [r_st], [r_WUP], out=WUP[0:64, :], in_=st[0:64, 0:512])
    st, r_st = C.wl_st[1], C.r_wl_st[1]
    kb.dma("sp", st[64:128, 0:512], W["rwkv_a_up"][0, d], [], [r_st])
    kb.op("dve", "tensor_copy", [r_st], [r_AUP], out=AUP[64:128, :], in_=st[64:128, 0:512])

    alloc_norm_tmps(C, A)
    xs = [A.f32(D) for _ in range(2)]
    r_xs = rr(2)
    hx = xs[0]
    r_hx = r_xs[0]
    hxb = C.nt_hb[0]
    r_hxb = C.r_nt_hb[0]
    hx_ss = A.f32(1)
    hx_rs = A.f32(1)
    r_hxs = Res()
    hT = v3(A.bf(8 * 514), 8)
    r_hT = Res()
    Pb = [A.f32(514) for _ in range(2)]
    r_Pb = rr(2)
    Rr = A.f32(512)
    Kk = A.f32(512)
    r_Rr, r_Kk = Res(), Res()
    TWD = A.bf(512)
    AD = A.bf(512)
    r_TWD, r_AD = Res(), Res()
    tmp = [A.f32(512) for _ in range(11)]
    r_tmp = rr(11)
    SQ = A.bf(512)
    PRb = A.bf(512)
    r_SQ, r_PRb = Res(), Res()
    VTp = [A.bf(576) for _ in range(4)]
    r_VTp = rr(4)
    AK = [v3(A.bf(1024), 8) for _ in range(4)]
    KR = [v3(A.bf(1024), 8) for _ in range(4)]
    r_AK, r_KR = rr(4), rr(4)
    Kc = [v3(A.bf(1024), 8) for _ in range(4)]
    r_Kc = rr(4)
    KTS = [v3(A.bf(1024), 8) for _ in range(2)]
    r_KTS = rr(2)
    M12 = [[v3(A.bf(1024), 8) for _ in range(2)] for _ in range(4)]
    r_M12 = [rr(2) for _ in range(4)]
    LW = [v3(A.bf(512), 8) for _ in range(4)]
    r_LW = rr(4)
    X = v4(A.bf(8 * 4 * 128), 8, 4)
    r_XU = rr(8)
    r_XV = rr(4)
    U0 = v4(A.bf(8 * 4 * 128), 8, 4)
    r_U0 = rr(4)
    DG = v4(A.bf(4 * 8 * 128), 4, 8)
    r_DG = Res()
    QsS = [[v4(A.bf(1024), 8, 2) for _ in range(2)] for _ in range(2)]
    QTsS = [[v4(A.bf(1024), 8, 2) for _ in range(2)] for _ in range(2)]
    r_QsS, r_QTsS = [rr(2), rr(2)], [rr(2), rr(2)]
    PTS = [v4(A.bf(1024), 8, 2) for _ in range(2)]
    r_PTS = rr(2)
    BVsS = [v4(A.bf(1024), 8, 2) for _ in range(2)]
    r_BVsS = rr(2)
    Rbd = [v3(A.bf(512), 4) for _ in range(2)]
    r_Rbd = rr(2)
    YT = [C.wl_st[0][:, 0:512], C.wl_st[1][:, 0:512]]
    r_YT = C.r_wl_st
    GAMALL = v3(A.f32(4 * (NCH + 1)), 4)
    r_GAM = Res()
    GP = v3(A.f32(32), 4)
    r_GP = Res()
    rkS = A.f32(32)
    r_rkS = Res()
    bo0 = A.f32(512)
    BO = [bo0, bo0]
    r_bo0 = Res()
    r_BO = [r_bo0, r_bo0]

    kb.op("pool", "memset", [], [r_GAM], ap=GAMALL, constant=0.0)
    for b in range(2):
        kb.op("pool", "memset", [], [r_Rbd[b]], ap=Rbd[b], constant=0.0)
    for i in range(4):
        kb.op("pool", "memset", [], [r_VTp[i]], ap=VTp[i], constant=0.0)
    kb.op("pool", "memset", [], [r_hT], ap=hT, constant=0.0)

    tiles = list(range(NT)) if d == 0 else list(range(NT - 1, -1, -1))
    xi = 0
    yi = 0
    cur = 0
    for t in tiles:
        t0 = t * 512
        kb.op("pool", "memset", [], [r_hx], ap=hx[0:2, :], constant=0.0)
        if t0 - 1 >= 0:
            kb.dma("sp", hx[0:1, :], x_in[t0 - 1:t0, :], [], [r_hx])
        if t0 + 512 < T:
            kb.dma("sp", hx[1:2, :], x_in[t0 + 512:t0 + 513, :], [], [r_hx])
        kb.op("act", "activation", [r_hx], [r_hxb, r_hxs], out=hxb[0:2, :], in_=hx[0:2, :], func=AF.Square,
              accum_out=hx_ss[0:2, :])
        kb.op("act", "activation", [r_hxs], [r_hxs], out=hx_rs[0:2, :], in_=hx_ss[0:2, :], func=AF.Sqrt,
              scale=1.0 / D, bias=EPS)
        kb.op("dve", "reciprocal", [r_hxs], [r_hxs], out=hx_rs[0:2, :], in_=hx_rs[0:2, :])
        kb.op("dve", "tensor_tensor", [r_hxs, r_tfl], [r_hxs], out=hx_rs[0:2, :], in0=hx_rs[0:2, :],
              in1=tfl[0:2, t:t + 1], op=ALU.mult)
        kb.op("dve", "tensor_scalar", [r_hx, r_hxs], [r_hxb], out=hxb[0:2, :], in0=hx[0:2, :],
              scalar1=hx_rs[0:2, :], scalar2=None, op0=ALU.mult)
        b = gp.get()
        pth = v3(psb[b][:, 0:16], 8)
        for c in range(8):
            kb.op("pe", "transpose", [r_hxb, r_ident], [r_ps[b]], inc=(c == 7),
                  out=pth[:, c, :], in_=hxb[0:2, c * 128:(c + 1) * 128], identity=ident[0:2, 0:2])
        kb.op("dve", "tensor_copy", [r_ps[b]], [r_hT], out=hT[:, :, 0:1], in_=pth[:, :, 0:1])
        kb.op("dve", "tensor_copy", [r_ps[b]], [r_hT], out=hT[:, :, 513:514], in_=pth[:, :, 1:2])
        for j in range(4):
            r0 = t0 + j * 128
            bx = xi % 2
            xi += 1
            kb.dma("sp", xs[bx], x_in[r0:r0 + 128, :], [], [r_xs[bx]])
            norm_transpose(kb, C, xs[bx], r_xs[bx], hT, r_hT, 1 + j * 128)

        sh0 = 0 if d == 0 else 2

        def proj(ft, mucol, r_mu):
            bm = gp.get()
            for c in range(8):
                kb.op("pe", "matmul", [r_Wp[c], r_hT], [r_ps[bm]], inc=(c == 7), out=ps[bm],
                      lhsT=Wp[:, c, ft * 128:(ft + 1) * 128], rhs=hT[:, c, 1:513], start=(c == 0), stop=(c == 7))
            bh = gp.get()
            for c in range(8):
                kb.op("pe", "matmul", [r_Wp[c], r_hT], [r_ps[bh]], inc=(c == 7), out=ps[bh][:, 0:2],
                      lhsT=Wp[:, c, ft * 128:(ft + 1) * 128], rhs=hT[:, c, 0:514:513], start=(c == 0),
                      stop=(c == 7))
            k = proj.i % 2
            proj.i += 1
            P_, r_P = Pb[k], r_Pb[k]
            evac(P_[:, 1:513], ps[bm], [r_ps[bm]], [r_P])
            kb.op("dve", "tensor_copy", [r_ps[bh]], [r_P], out=P_[:, 0:514:513], in_=ps[bh][:, 0:2])
            return P_, r_P, mucol, r_mu

        proj.i = 0

        def shiftmix(out, r_out, pr):
            P_, r_P, mucol, r_mu = pr
            dd, r_dd = tmp[9], r_tmp[9]
            kb.op("pool", "tensor_tensor", [r_P], [r_dd], out=dd, in0=P_[:, sh0:sh0 + 512], in1=P_[:, 1:513],
                  op=ALU.subtract)
            kb.op("dve", "scalar_tensor_tensor", [r_dd, r_P, r_mu], [r_out], out=out, in0=dd, scalar=mucol,
                  in1=P_[:, 1:513], op0=ALU.mult, op1=ALU.add)

        pr = proj(12, mu_w[:, 0:1], r_mu_w)
        zwa, r_zwa = tmp[0], r_tmp[0]
        shiftmix(zwa, r_zwa, pr)
        kb.op("act", "activation", [r_zwa], [r_TWD], out=TWD[0:64, :], in_=zwa[0:64, :], func=AF.Tanh)
        kb.op("pool", "tensor_copy", [r_zwa], [r_AD], out=AD[64:128, :], in_=zwa[64:128, :])

        brk = 4
        stq = {}

        def pre(i):
            pr = proj(i, mu_r[:, i:i + 1], r_mu_r)
            shiftmix(Rr, r_Rr, pr)
            pr = proj(4 + i, mu_r[:, 4 + i:5 + i], r_mu_r)
            shiftmix(Kk, r_Kk, pr)
            pr = proj(8 + i, mu_r[:, 8 + i:9 + i], r_mu_r)
            shiftmix(VTp[i][:, 64:576], r_VTp[i], pr)
            SIG, Aa, KK, RN, L, LX, EP, EM, EX = tmp[0:9]
            r_SIG, r_Aa, r_KK, r_RN, r_L, r_LX, r_EP, r_EM, r_EX = r_tmp[0:9]
            bw = gp.get()
            kb.op("pe", "matmul", [r_WUP, r_TWD], [r_ps[bw]], out=ps[bw], lhsT=WUP[0:64, i * 128:(i + 1) * 128],
                  rhs=TWD[0:64, :], start=True, stop=True)
            kb.op("act", "activation", [r_ps[bw], r_w0c], [r_SIG], out=SIG, in_=ps[bw], func=AF.Sigmoid,
                  bias=w0c[:, i:i + 1])
            kb.op("dve", "tensor_scalar", [r_SIG], [r_SIG], out=SIG, in0=SIG, scalar1=LOGC, scalar2=None,
                  op0=ALU.mult)
            ba = gp.get()
            kb.op("pe", "matmul", [r_AUP, r_AD], [r_ps[ba]], out=ps[ba], lhsT=AUP[64:128, i * 128:(i + 1) * 128],
                  rhs=AD[64:128, :], start=True, stop=True)
            kb.op("act", "activation", [r_ps[ba], r_a0c], [r_Aa], out=Aa, in_=ps[ba], func=AF.Sigmoid,
                  bias=a0c[:, i:i + 1])
            kb.op("dve", "tensor_scalar", [r_Kk, r_kkc], [r_KK], out=KK, in0=Kk, scalar1=kkc[:, i:i + 1],
                  scalar2=None, op0=ALU.mult)
            kb.op("act", "activation", [r_KK], [r_SQ], out=SQ, in_=KK, func=AF.Square)
            bn = gp.get()
            kb.op("pe", "matmul", [r_bones, r_SQ], [r_ps[bn]], out=ps[bn], lhsT=bones, rhs=SQ, start=True, stop=True)
            kb.op("act", "activation", [r_ps[bn]], [r_RN], out=RN, in_=ps[bn], func=AF.Sqrt, bias=1e-24)
            kb.op("dve", "reciprocal", [r_RN], [r_RN], out=RN, in_=RN)
            kb.op("dve", "tensor_tensor", [r_KK, r_RN], [r_KK], out=KK, in0=KK, in1=RN, op=ALU.mult)
            if d == 0:
                kb.op("dve", "tensor_tensor_scan", [r_SIG, r_cmask], [r_L], out=L, data0=cmask, data1=SIG,
                      initial=0.0, op0=ALU.mult, op1=ALU.add)
            else:
                Lf, r_Lf = tmp[10], r_tmp[10]
                kb.op("dve", "tensor_tensor_scan", [r_SIG, r_cmask], [r_Lf], out=Lf, data0=cmask, data1=SIG,
                      initial=0.0, op0=ALU.mult, op1=ALU.add)
                kb.op("pool", "tensor_tensor", [r_SIG, r_Lf], [r_LX], out=LX, in0=SIG, in1=Lf, op=ALU.subtract)
                kb.op("dve", "tensor_tensor", [r_LX, r_Lf], [r_L], out=v3(L, 8), in0=v3(LX, 8),
                      in1=v3(Lf, 8)[:, :, 63:64].to_broadcast([128, 8, 64]), op=ALU.add)
            kb.op("pool", "tensor_tensor", [r_L, r_SIG], [r_LX], out=LX, in0=L, in1=SIG, op=ALU.subtract)
            kb.op("act", "activation", [r_L], [r_EP], out=EP, in_=L, func=AF.Exp)
            kb.op("act", "activation", [r_L], [r_EM], out=EM, in_=L, func=AF.Exp, scale=-1.0)
            kb.op("act", "activation", [r_LX], [r_EX], out=EX, in_=LX, func=AF.Exp)
            gcol = 63 if d == 0 else 0
            gdst = GAMALL[:, i, t * 8 + 1:t * 8 + 9] if d == 0 else GAMALL[:, i, t * 8:t * 8 + 8]
            kb.op("pool", "tensor_copy", [r_EP], [r_GAM], out=gdst, in_=v3(EP, 8)[:, :, gcol])
            kb.op("dve", "tensor_tensor", [r_KK, r_EX], [r_KR[i]], out=KR[i][:, :, 0:64], in0=v3(KK, 8),
                  in1=v3(EX, 8), op=ALU.mult)
            kb.op("dve", "tensor_tensor", [r_Rr, r_EP], [r_KR[i]], out=KR[i][:, :, 64:128], in0=v3(Rr, 8),
                  in1=v3(EP, 8), op=ALU.mult)
            kb.op("pool", "tensor_tensor", [r_KK, r_Aa], [r_RN], out=RN, in0=KK, in1=Aa, op=ALU.mult)
            kb.op("dve", "tensor_tensor", [r_RN, r_EM], [r_AK[i]], out=AK[i][:, :, 0:64], in0=v3(RN, 8),
                  in1=v3(EM, 8), op=ALU.mult)
            kb.op("dve", "tensor_scalar", [r_Aa, r_kac], [r_Aa], out=Aa, in0=Aa, scalar1=-1.0,
                  scalar2=kac[:, i:i + 1], op0=ALU.add, op1=ALU.mult)
            kb.op("dve", "scalar_tensor_tensor", [r_Aa, r_Kk], [r_LX], out=LX, in0=Aa, scalar=1.0, in1=Kk,
                  op0=ALU.add, op1=ALU.mult)
            kb.op("dve", "tensor_tensor", [r_LX, r_EM], [r_AK[i]], out=AK[i][:, :, 64:128], in0=v3(LX, 8),
                  in1=v3(EM, 8), op=ALU.mult)
            kb.op("pool", "tensor_tensor", [r_Rr, r_LX], [r_EX], out=EX, in0=Rr, in1=LX, op=ALU.mult)
            kb.op("dve", "tensor_scalar", [r_EX, r_rkc], [r_PRb], out=PRb, in0=EX, scalar1=rkc[:, i:i + 1],
                  scalar2=None, op0=ALU.mult)
            for j in range(4):
                kb.op("pe", "matmul", [r_PRb, r_hind], [r_ps[brk]], inc=(j == 3),
                      out=ps[brk][:, j * 8:(j + 1) * 8], lhsT=PRb[:, j * 128:(j + 1) * 128],
                      rhs=hind[:, i * 8:(i + 1) * 8], start=(i == 0 and j == 0), stop=(i == 3))

            for e in range(2):
                hp = slice(64 * e, 64 * e + 64)
                for half in range(2):
                    bg = gp.get()
                    for cc in range(4):
                        c = half * 4 + cc
                        kb.op("pe", "matmul", [r_AK[i], r_KR[i]], [r_ps[bg]], inc=(cc == 3),
                              out=ps[bg][:, cc * 128:(cc + 1) * 128], lhsT=AK[i][hp, c, :], rhs=KR[i][hp, c, :],
                              start=True, stop=True)
                    kb.op("dve", "tensor_tensor", [r_ps[bg], r_m12], [r_M12[i][e]],
                          out=M12[i][e][:, half * 4:half * 4 + 4, :], in0=v3(ps[bg], 4),
                          in1=m12.unsqueeze(1).to_broadcast([128, 4, 128]), op=ALU.mult)
            q, qt = 0, 0
            for e in range(2):
                hp = slice(64 * e, 64 * e + 64)
                bg = gp.get()
                for c in range(8):
                    kb.op("pe", "matmul", [r_AK[i], r_KR[i]], [r_ps[bg]], inc=(c == 7),
                          out=ps[bg][:, c * 64:(c + 1) * 64],
                          lhsT=KR[i][hp, c, :], rhs=AK[i][hp, c, 0:64], start=True, stop=True)
                kb.op("dve", "scalar_tensor_tensor", [r_ps[bg], r_ma], [r_QsS[i % 2][q]],
                      out=QsS[i % 2][q][0:64, :, e, :], in0=v3(ps[bg][0:64, :], 8), scalar=-1.0,
                      in1=ma[0:64, :].unsqueeze(1).to_broadcast([64, 8, 64]), op0=ALU.mult, op1=ALU.mult)
            for e in range(2):
                kb.op("dve", "tensor_scalar", [r_M12[i][e]], [r_QTsS[i % 2][qt]], out=QTsS[i % 2][qt][0:64, :, e, :],
                      in0=M12[i][e][0:64, :, 0:64], scalar1=-1.0, scalar2=None, op0=ALU.mult)
            kb.op("dve", "tensor_tensor", [r_QTsS[i % 2][qt], r_ident], [r_PTS[i % 2]],
                  out=PTS[i % 2][0:64].rearrange("p a b c -> p (a b) c"),
                  in0=QTsS[i % 2][qt][0:64].rearrange("p a b c -> p (a b) c"),
                  in1=i64.unsqueeze(1).to_broadcast([64, 16, 64]), op=ALU.add)
            stq[i] = [q, qt]

        def dbl(i, lvl):
            q, qt = stq[i]
            qn, qtn = 1 - q, 1 - qt
            for half in range(2):
                bg = gp.get()
                for cc in range(4):
                    c = half * 4 + cc
                    for e in range(2):
                        kb.op("pe", "matmul", [r_QsS[i % 2][q], r_QTsS[i % 2][qt]], [r_ps[bg]], inc=(cc == 3 and e == 1),
                              out=ps[bg][0:64, (cc * 2 + e) * 64:(cc * 2 + e + 1) * 64],
                              lhsT=QTsS[i % 2][qt][0:64, c, e, :], rhs=QsS[i % 2][q][0:64, c, e, :], start=True, stop=True)
                evac(QsS[i % 2][qn][0:64, half * 4:half * 4 + 4].rearrange("p a b c -> p (a b c)"), ps[bg][0:64, :],
                     [r_ps[bg]], [r_QsS[i % 2][qn]])
            if lvl < 5:
                for half in range(2):
                    bg = gp.get()
                    for cc in range(4):
                        c = half * 4 + cc
                        for e in range(2):
                            kb.op("pe", "matmul", [r_QsS[i % 2][q], r_QTsS[i % 2][qt]], [r_ps[bg]], inc=(cc == 3 and e == 1),
                                  out=ps[bg][0:64, (cc * 2 + e) * 64:(cc * 2 + e + 1) * 64],
                                  lhsT=QsS[i % 2][q][0:64, c, e, :], rhs=QTsS[i % 2][qt][0:64, c, e, :], start=True, stop=True)
                    evac(QTsS[i % 2][qtn][0:64, half * 4:half * 4 + 4].rearrange("p a b c -> p (a b c)"),
                         ps[bg][0:64, :], [r_ps[bg]], [r_QTsS[i % 2][qtn]])
            for half in range(2):
                bg = gp.get()
                for cc in range(4):
                    c = half * 4 + cc
                    for e in range(2):
                        kb.op("pe", "matmul", [r_QsS[i % 2][qn], r_PTS[i % 2]], [r_ps[bg]], inc=(cc == 3 and e == 1),
                              out=ps[bg][0:64, (cc * 2 + e) * 64:(cc * 2 + e + 1) * 64],
                              lhsT=QsS[i % 2][qn][0:64, c, e, :], rhs=PTS[i % 2][0:64, c, e, :], start=True, stop=True)
                pv = PTS[i % 2][0:64, half * 4:half * 4 + 4].rearrange("p a b c -> p (a b c)")
                kb.op("dve", "tensor_tensor", [r_ps[bg], r_PTS[i % 2]], [r_PTS[i % 2]], out=pv, in0=pv, in1=ps[bg][0:64, :],
                      op=ALU.add)
            q, qt = qn, qtn
            stq[i] = [q, qt]

        def post(i):
            q, qt = stq[i]
            bt = gp.get()
            for c in range(8):
                kb.op("pe", "transpose", [r_AK[i], r_ident], [r_ps[bt]], inc=(c == 7),
                      out=psb[bt][:, c * 128:(c + 1) * 128], in_=AK[i][:, c, :], identity=ident)
            evac(Kc[i], v3(psb[bt], 8), [r_ps[bt]], [r_Kc[i]])
            bt = gp.get()
            for c in range(8):
                kb.op("pe", "transpose", [r_KR[i], r_ident], [r_ps[bt]], inc=(c == 7),
                      out=psb[bt][0:64, c * 128:(c + 1) * 128], in_=KR[i][:, c, 0:64], identity=ident)
            evac(KTS[i % 2][0:64], v3(psb[bt][0:64, :], 8), [r_ps[bt]], [r_KTS[i % 2]])
            bt = gp.get()
            for c in range(8):
                kb.op("pe", "transpose", [r_VTp[i], r_ident], [r_ps[bt]], inc=(c == 7),
                      out=psb[bt][:, c * 128:(c + 1) * 128], in_=VTp[i][:, c * 64:c * 64 + 128], identity=ident)
            evac(X[64:128, :, i, :], v3(psb[bt][64:128, :], 8), [r_ps[bt]], [r_XV[i]])
            for half in range(2):
                bg = gp.get()
                for cc in range(4):
                    c = half * 4 + cc
                    kb.op("pe", "matmul", [r_KTS[i % 2], r_PTS[i % 2]], [r_ps[bg]], inc=(cc == 3),
                          out=ps[bg][:, cc * 128:(cc + 1) * 128], lhsT=KTS[i % 2][0:64, c, :],
                          rhs=PTS[i % 2][0:64, c].rearrange("p a b -> p (a b)"), start=True, stop=True)
                for e in range(2):
                    hp = slice(64 * e, 64 * e + 64)
                    evac(LW[i][hp, half * 4:half * 4 + 4, :], v3(ps[bg], 4)[hp, :, 64 * e:64 * e + 64],
                         [r_ps[bg]], [r_LW[i]], scale=-1.0)
            for half in range(2):
                bg = gp.get()
                for cc in range(4):
                    c = half * 4 + cc
                    for e in range(2):
                        kb.op("pe", "matmul", [r_M12[i][e], r_XV[i]], [r_ps[bg]], inc=(cc == 3 and e == 1),
                              out=ps[bg][:, (cc * 2 + e) * 64:(cc * 2 + e + 1) * 64],
                              lhsT=M12[i][e][64:128, c, :], rhs=X[64:128, c, i, 64 * e:64 * e + 64],
                              start=True, stop=True)
                evac(BVsS[i % 2][0:64, half * 4:half * 4 + 4].rearrange("p a b c -> p (a b c)"), ps[bg][0:64, :],
                     [r_ps[bg]], [r_BVsS[i % 2]])
            for half in range(2):
                bg = gp.get()
                for cc in range(4):
                    c = half * 4 + cc
                    for e in range(2):
                        kb.op("pe", "matmul", [r_PTS[i % 2], r_BVsS[i % 2]], [r_ps[bg]], inc=(cc == 3 and e == 1),
                              out=ps[bg][0:64, (cc * 2 + e) * 64:(cc * 2 + e + 1) * 64],
                              lhsT=PTS[i % 2][0:64, c, e, :], rhs=BVsS[i % 2][0:64, c, e, :], start=True, stop=True)
                evac(U0[0:64, half * 4:half * 4 + 4, i, :], v3(ps[bg][0:64, :], 4), [r_ps[bg]], [r_U0[i]], scale=-1.0)

        for g2 in range(2):
            for i in (2 * g2, 2 * g2 + 1):
                pre(i)
            for lvl in range(1, 6):
                for i in (2 * g2, 2 * g2 + 1):
                    dbl(i, lvl)
            for i in (2 * g2, 2 * g2 + 1):
                post(i)

        evac(rkS, ps[brk][:, 0:32], [r_ps[brk]], [r_rkS])
        for j in range(4):
            bo = BO[j % 2]
            r_bo = r_BO[j % 2]
            bt = gp.get()
            for i in range(4):
                kb.op("pe", "transpose", [r_VTp[i], r_ident], [r_ps[bt]], inc=(i == 3),
                      out=psb[bt][:, i * 128:(i + 1) * 128], in_=VTp[i][:, 64 + j * 128:64 + (j + 1) * 128],
                      identity=ident)
            kb.op("dve", "tensor_tensor", [r_ps[bt], r_rkS], [r_bo], out=v3(bo, 8), in0=v3(psb[bt][:, 0:512], 8),
                  in1=rkS[:, j * 8:(j + 1) * 8].unsqueeze(2).to_broadcast([128, 8, 64]), op=ALU.mult)
            kb.dma("pool", BD[t0 + j * 128:t0 + (j + 1) * 128, :], bo, [r_bo], [])

        gsrc = GAMALL[:, :, t * 8:t * 8 + 8] if d == 0 else GAMALL[:, :, t * 8 + 1:t * 8 + 9]
        kb.op("dve", "tensor_tensor", [r_GAM, r_cfl], [r_GP], out=GP, in0=gsrc,
              in1=cfl[:, t * 8:t * 8 + 8].unsqueeze(1).to_broadcast([128, 4, 8]), op=ALU.mult)
        for i in range(4):
            kb.op("dve", "tensor_tensor", [r_LW[i], r_GP], [r_LW[i]], out=LW[i], in0=LW[i],
                  in1=GP[:, i, :].unsqueeze(2).to_broadcast([128, 8, 64]), op=ALU.mult)
            kb.op("pool", "tensor_tensor", [r_KR[i], r_GP], [r_KR[i]], out=KR[i][:, :, 64:128],
                  in0=KR[i][:, :, 64:128], in1=GP[:, i, :].unsqueeze(2).to_broadcast([128, 8, 64]), op=ALU.mult)
        kb.op("dve", "tensor_tensor", [r_GP, r_ident], [r_DG], out=DG.rearrange("p a b c -> p (a b) c"),
              in0=ident.unsqueeze(1).to_broadcast([128, 32, 128]),
              in1=GP.rearrange("p a b -> p (a b)").unsqueeze(2).to_broadcast([128, 32, 128]), op=ALU.mult)

        chunks = list(range(8)) if d == 0 else list(range(7, -1, -1))
        for c in chunks:
            nxt = 1 - cur
            R0, r_R0 = Rbd[cur], r_Rbd[cur]
            R1, r_R1 = Rbd[nxt], r_Rbd[nxt]
            bu = 5
            for i in range(4):
                kb.op("pe", "matmul", [r_LW[i], r_R0], [r_ps[bu]], inc=(i == 3),
                      out=ps[bu][0:64, i * 128:(i + 1) * 128], lhsT=LW[i][:, c, :], rhs=R0[:, i, :],
                      start=True, stop=True)
            kb.op("dve", "tensor_tensor", [r_ps[bu]] + r_U0, [r_XU[c]], out=X[0:64, c, :, :],
                  in0=v3(ps[bu][0:64, :], 4), in1=U0[0:64, c, :, :], op=ALU.add)
            bs = 6
            for i in range(4):
                kb.op("pe", "matmul", [r_DG, r_R0], [r_ps[bs]], inc=False,
                      out=ps[bs][:, i * 128:(i + 1) * 128], lhsT=DG[:, i, c, :], rhs=R0[:, i, :],
                      start=True, stop=False)
                kb.op("pe", "matmul", [r_Kc[i], r_XU[c], r_XV[i]], [r_ps[bs]], inc=(i == 3),
                      out=ps[bs][:, i * 128:(i + 1) * 128], lhsT=Kc[i][:, c, :], rhs=X[:, c, i, :],
                      start=False, stop=True)
            kb.op("dve", "tensor_tensor", [r_ps[bs], r_bdm], [r_R1], out=R1.rearrange("p a b -> p (a b)"),
                  in0=ps[bs], in1=bdm, op=ALU.mult)
            by = 7
            for i in range(4):
                kb.op("pe", "matmul", [r_KR[i], r_R0], [r_ps[by]], inc=False,
                      out=ps[by][0:64, i * 128:(i + 1) * 128], lhsT=KR[i][:, c, 64:128], rhs=R0[:, i, :],
                      start=True, stop=False)
                for e in range(2):
                    kb.op("pe", "matmul", [r_M12[i][e], r_XU[c], r_XV[i]], [r_ps[by]], inc=(i == 3 and e == 1),
                          out=ps[by][0:64, i * 128 + 64 * e:i * 128 + 64 * e + 64], lhsT=M12[i][e][:, c, 64:128],
                          rhs=X[:, c, i, 64 * e:64 * e + 64], start=False, stop=(e == 1))
            yb = yi % 2
            yi += 1
            kb.op("act", "activation", [r_ps[by]], [r_YT[yb]], out=YT[yb][0:64, :], in_=ps[by][0:64, :], func=AF.Copy)
            ct0 = t0 + c * 64
            kb.dma("pool", YD[ct0:ct0 + 64, :], YT[yb][0:64, :], [r_YT[yb]], [])
            cur = nxt


GELU_C = 1.5957691216057308
GN_EPS = 64e-5


def gelu_tanh(kb, src, r_src, out, r_out, t1, r_t1, t2, r_t2):
    kb.op("act", "activation", [r_src], [r_t1], out=t1, in_=src, func=AF.Square)
    kb.op("dve", "tensor_scalar", [r_t1], [r_t1], out=t1, in0=t1, scalar1=0.044715, scalar2=1.0,
          op0=ALU.mult, op1=ALU.add)
    kb.op("dve", "tensor_tensor", [r_t1, r_src], [r_t1], out=t1, in0=t1, in1=src, op=ALU.mult)
    kb.op("act", "activation", [r_t1], [r_t2], out=t2, in_=t1, func=AF.Sigmoid, scale=GELU_C)
    kb.op("dve", "tensor_tensor", [r_t2, r_src], [r_out], out=out, in0=t2, in1=src, op=ALU.mult)


def bcload(kb, A, src, cols):
    t = A.f32(cols)
    r = Res()
    kb.dma("sp", t, src.partition_broadcast(128), [], [r])
    return t, r


def phase_mixc(kb, C, T, x_in, x_out, W, YD, BD):
    A = phase_begin(kb, C)
    ps, psb, r_ps = C.ps, C.ps_bf, C.r_ps
    ident, r_ident = C.ident, C.r_ident
    gt, r_gt = load_gcols(kb, A, W["mix_norm"][0], 8)
    Wp = v3(A.bf(8 * 1536), 8)
    r_Wp = rr(8)
    Wo = v3(A.bf(8 * 1024), 8)
    r_Wo = rr(8)
    alloc_wl(C, A)
    wab = W["ab_w_in"][0]
    for c in range(8):
        rows = wab[c * 128:(c + 1) * 128, :]
        load_weight(kb, C, Wp[:, c, 0:1024], r_Wp[c], rows[:, 0:1024], 1024, gt[:, c:c + 1], extra=[r_gt])
        load_weight(kb, C, Wp[:, c, 1024:1536], r_Wp[c], rows[:, 2560:3072], 512, gt[:, c:c + 1], extra=[r_gt])
        load_weight(kb, C, Wo[:, c, :], r_Wo[c], W["ab_w_out"][0][c * 128:(c + 1) * 128, :], 1024, 1.0)
    sgn, r_sgn = bcload(kb, A, W["sgu_norm"][0], 512)
    gnw, r_gnw = bcload(kb, A, W["rwkv_gn_w"][0], 512)
    gnb, r_gnb = bcload(kb, A, W["rwkv_gn_b"][0], 512)
    bcol = A.f32(4)
    r_bcol = Res()
    kb.dma("sp", bcol, W["sgu_b"][0].rearrange("g q -> q g"), [], [r_bcol], allow_slow_non_contiguous=True)
    wsT = v3(A.bf(512), 4)
    r_wsT = Res()
    for g in range(4):
        st, r_st = C.wl_st[g % 2], C.r_wl_st[g % 2]
        kb.dma("sp", st[:, 0:128], W["sgu_w"][0, g], [], [r_st])
        sb = A.bf(128)
        r_sb = Res()
        kb.op("dve", "tensor_copy", [r_st], [r_sb], out=sb, in_=st[:, 0:128])
        kb.op("pe", "transpose", [r_sb, r_ident], [r_ps[4]], out=psb[4][:, 0:128], in_=sb, identity=ident)
        kb.op("dve", "tensor_copy", [r_ps[4]], [r_wsT], out=wsT[:, g, :], in_=psb[4][:, 0:128])
    alloc_norm_tmps(C, A)
    xs = [A.f32(D) for _ in range(2)]
    r_xs = rr(2)
    xr = [A.f32(D) for _ in range(2)]
    r_xr = rr(2)
    xo = [A.f32(D) for _ in range(2)]
    r_xo = rr(2)
    hT = v3(A.bf(8 * 512), 8)
    r_hT = Res()
    tm = [A.f32(512) for _ in range(8)]
    r_tm = rr(8)
    ld = [[A.f32(512) for _ in range(4)] for _ in range(2)]
    r_ld = [rr(4) for _ in range(2)]
    vn = A.bf(512)
    r_vn = Res()
    yab = A.bf(1024)
    r_yab = Res()
    yT = v3(A.bf(1024), 8)
    r_yT = Res()
    st8 = [A.f32(8) for _ in range(3)]
    r_st8 = rr(3)
    ss1 = A.f32(1)
    rs1 = A.f32(1)
    r_s1 = Res()
    NT = T // 512
    xi = 0
    si = 0
    for t in range(NT):
        for j in range(4):
            r0 = t * 512 + j * 128
            bx = xi % 2
            xi += 1
            kb.dma("sp", xs[bx], x_in[r0:r0 + 128, :], [], [r_xs[bx]])
            norm_transpose(kb, C, xs[bx], r_xs[bx], hT, r_hT, j * 128)
        for j in range(4):
            r0 = t * 512 + j * 128
            k = si % 2
            si += 1
            L4, r_L4 = ld[k], r_ld[k]
            for n, src in enumerate((YD[0], YD[1], BD[0], BD[1])):
                kb.dma("sp", L4[n], src[r0:r0 + 128, :], [], [r_L4[n]])
            kb.dma("sp", xr[k], x_in[r0:r0 + 128, :], [], [r_xr[k]])
            for n in range(3):
                for c in range(8):
                    kb.op("pe", "matmul", [r_Wp[c], r_hT], [r_ps[n]], inc=(c == 7), out=ps[n],
                          lhsT=hT[:, c, j * 128:(j + 1) * 128], rhs=Wp[:, c, n * 512:(n + 1) * 512],
                          start=(c == 0), stop=(c == 7))
            gu, gv = tm[0], tm[1]
            gelu_tanh(kb, ps[0], r_ps[0], gu, r_tm[0], tm[2], r_tm[2], tm[3], r_tm[3])
            gelu_tanh(kb, ps[1], r_ps[1], gv, r_tm[1], tm[4], r_tm[4], tm[5], r_tm[5])
            kb.op("act", "activation", [r_tm[1]], [r_tm[4], r_s1], out=tm[4], in_=gv, func=AF.Square, accum_out=ss1)
            rstd_ops(kb, ss1, rs1, r_s1, r_s1, 512, EPS)
            kb.op("dve", "scalar_tensor_tensor", [r_tm[1], r_s1, r_sgn], [r_vn], out=vn, in0=gv, scalar=rs1,
                  in1=sgn, op0=ALU.mult, op1=ALU.mult)
            for g in range(4):
                kb.op("pe", "matmul", [r_wsT, r_vn], [r_ps[3]], inc=(g == 3), out=ps[3][:, g * 128:(g + 1) * 128],
                      lhsT=wsT[:, g, :], rhs=vn[:, g * 128:(g + 1) * 128], start=(g == 0), stop=(g == 3))
            for g in range(4):
                kb.op("dve", "scalar_tensor_tensor", [r_ps[3], r_bcol, r_tm[0]], [r_yab],
                      out=yab[:, g * 128:(g + 1) * 128], in0=ps[3][:, g * 128:(g + 1) * 128],
                      scalar=bcol[:, g:g + 1], in1=gu[:, g * 128:(g + 1) * 128], op0=ALU.add, op1=ALU.mult)
            y, yc, sq = tm[2], tm[3], tm[5]
            kb.op("pool", "tensor_tensor", [r_L4[0], r_L4[1]], [r_tm[2]], out=y, in0=L4[0], in1=L4[1], op=ALU.add)
            kb.op("dve", "tensor_reduce", [r_tm[2]], [r_st8[0]], out=st8[0], in_=v3(y, 8), axis=AX.X, op=ALU.add)
            kb.op("dve", "tensor_scalar", [r_st8[0]], [r_st8[0]], out=st8[0], in0=st8[0], scalar1=1.0 / 64,
                  scalar2=None, op0=ALU.mult)
            kb.op("dve", "tensor_tensor", [r_tm[2], r_st8[0]], [r_tm[3]], out=v3(yc, 8), in0=v3(y, 8),
                  in1=st8[0].unsqueeze(2).to_broadcast([128, 8, 64]), op=ALU.subtract)
            kb.op("act", "activation", [r_tm[3]], [r_tm[5]], out=sq, in_=yc, func=AF.Square)
            kb.op("dve", "tensor_reduce", [r_tm[5]], [r_st8[1]], out=st8[1], in_=v3(sq, 8), axis=AX.X, op=ALU.add)
            kb.op("act", "activation", [r_st8[1]], [r_st8[2]], out=st8[2], in_=st8[1], func=AF.Sqrt,
                  scale=1.0 / 64, bias=GN_EPS)
            kb.op("dve", "reciprocal", [r_st8[2]], [r_st8[2]], out=st8[2], in_=st8[2])
            kb.op("dve", "tensor_tensor", [r_tm[3], r_st8[2]], [r_tm[3]], out=v3(yc, 8), in0=v3(yc, 8),
                  in1=st8[2].unsqueeze(2).to_broadcast([128, 8, 64]), op=ALU.mult)
            kb.op("pool", "tensor_tensor", [r_tm[3], r_gnw], [r_tm[3]], out=yc, in0=yc, in1=gnw, op=ALU.mult)
            kb.op("pool", "tensor_tensor", [r_tm[3], r_gnb], [r_tm[3]], out=yc, in0=yc, in1=gnb, op=ALU.add)
            kb.op("pool", "tensor_tensor", [r_L4[2], r_L4[3]], [r_tm[5]], out=sq, in0=L4[2], in1=L4[3], op=ALU.add)
            kb.op("pool", "tensor_tensor", [r_tm[3], r_tm[5]], [r_tm[3]], out=yc, in0=yc, in1=sq, op=ALU.add)
            kb.op("act", "activation", [r_ps[2]], [r_tm[4]], out=tm[4], in_=ps[2], func=AF.Sigmoid)
            kb.op("dve", "tensor_tensor", [r_tm[3], r_tm[4]], [r_yab], out=yab[:, 512:1024], in0=yc, in1=tm[4],
                  op=ALU.mult)
            for c in range(8):
                kb.op("pe", "transpose", [r_yab, r_ident], [r_ps[4]], inc=(c == 7),
                      out=psb[4][:, c * 128:(c + 1) * 128], in_=yab[:, c * 128:(c + 1) * 128], identity=ident)
            kb.op("act", "activation", [r_ps[4]], [r_yT], out=yT, in_=v3(psb[4], 8), func=AF.Copy)
            for hf in range(2):
                bo = 5 if hf == 0 else 3
                for c in range(8):
                    kb.op("pe", "matmul", [r_yT, r_Wo[c]], [r_ps[bo]], inc=(c == 7), out=ps[bo],
                          lhsT=yT[:, c, :], rhs=Wo[:, c, hf * 512:(hf + 1) * 512], start=(c == 0), stop=(c == 7))
                kb.op("dve", "tensor_tensor", [r_ps[bo], r_xr[k]], [r_xo[k]], out=xo[k][:, hf * 512:(hf + 1) * 512],
                      in0=ps[bo], in1=xr[k][:, hf * 512:(hf + 1) * 512], op=ALU.add)
            kb.dma("pool", x_out[r0:r0 + 128, :], xo[k], [r_xo[k]], [])


PAD = 1024
NVT = 45
SLOPES = [2.0 ** (-8.0 * (h + 1) / 16.0) for h in range(16)]


def attn_consts(T, seg):
    c = {}
    p = np.arange(128)[:, None]
    for name, dil, nq in (("d1", 1, 128), ("d4", 4, 128), ("d16", 16, 32)):
        q = np.arange(nq)[None, :]
        for kb in range(2):
            ik = -64 + 128 * kb + p
            dist = np.abs(ik - q)
            c[f"{name}k{kb}"] = np.where(dist <= 64, -(dil * dist).astype(np.float32), np.float32(-1e30)).astype(np.float32)
    NT = T // 512
    vfl = np.zeros((128, NT, NVT), np.float32)
    pp = np.arange(128)
    for t in range(NT):
        t0 = t * 512
        s0 = (t0 // seg) * seg
        idx = 0

        def put(tok):
            nonlocal idx
            vfl[:, t, idx] = ((tok >= s0) & (tok < s0 + seg)).astype(np.float32)
            idx += 1
        for m in range(5):
            put(t0 - 64 + 128 * m + pp)
        for r in range(4):
            for kb in range(2):
                put(t0 + r + 4 * (-64 + 128 * kb + pp))
        for r in range(16):
            for kb in range(2):
                put(t0 + r + 16 * (-64 + 128 * kb + pp))
    c["vfl"] = vfl.reshape(128, NT * NVT)
    return c


ATT_SHAPES = {"d1k0": [128, 128], "d1k1": [128, 128], "d4k0": [128, 128], "d4k1": [128, 128],
              "d16k0": [128, 32], "d16k1": [128, 32]}


def phase_attn_qkv(kb, C, T, x_in, W, QT, KT, VD):
    A = phase_begin(kb, C)
    ps, psb, r_ps = C.ps, C.ps_bf, C.r_ps
    gt, r_gt = load_gcols(kb, A, W["mix_norm"][1], 8)
    Wp = v3(A.bf(8 * 3072), 8)
    r_Wp = rr(8)
    alloc_wl(C, A)
    for c in range(8):
        load_weight(kb, C, Wp[:, c, :], r_Wp[c], W["attn_w_in"][0][c * 128:(c + 1) * 128, :], 3072, gt[:, c:c + 1],
                    extra=[r_gt])
    bones, r_bones = cload(kb, A, C.cd["bones"], 128, bf=True)
    gq = A.f32(2)
    r_gq = Res()
    for e in range(2):
        kb.dma("sp", gq[64 * e:64 * e + 64, 0:1], W["attn_q_norm"][0].rearrange("(e o) -> e o", o=1), [], [r_gq])
        kb.dma("sp", gq[64 * e:64 * e + 64, 1:2], W["attn_k_norm"][0].rearrange("(e o) -> e o", o=1), [], [r_gq])
    kb.op("dve", "tensor_scalar", [r_gq], [r_gq], out=gq[:, 0:1], in0=gq[:, 0:1], scalar1=0.125, scalar2=None,
          op0=ALU.mult)
    zt = A.bf(1040)
    r_zt = Res()
    kb.op("pool", "memset", [], [r_zt], ap=zt, constant=0.0)
    for c in range(8):
        for side in range(2):
            c0 = 0 if side == 0 else PAD + T
            kb.dma("pool", KT[c * 128:(c + 1) * 128, c0:c0 + PAD], zt[:, 0:PAD], [r_zt], [])
            kb.dma("pool", VD[c0 + c * 128:c0 + (c + 1) * 128, :], zt, [r_zt], [])
    alloc_norm_tmps(C, A)
    xs = [A.f32(D) for _ in range(2)]
    r_xs = rr(2)
    hT = v3(A.bf(8 * 512), 8)
    r_hT = Res()
    sq = [A.bf(512) for _ in range(2)]
    r_sq = rr(2)
    rn = [A.f32(512) for _ in range(2)]
    r_rn = rr(2)
    qo = [A.bf(512) for _ in range(3)]
    r_qo = rr(3)
    va = [A.bf(1040) for _ in range(2)]
    r_va = rr(2)
    for b in range(2):
        kb.op("pool", "memset", [], [r_va[b]], ap=va[b], constant=1.0)
    NT = T // 512
    xi = qi = vi = 0
    gp = PsPool([0, 1, 2, 3, 4, 5])
    for t in range(NT):
        t0 = t * 512
        for j in range(4):
            r0 = t0 + j * 128
            bx = xi % 2
            xi += 1
            kb.dma("sp", xs[bx], x_in[r0:r0 + 128, :], [], [r_xs[bx]])
            norm_transpose(kb, C, xs[bx], r_xs[bx], hT, r_hT, j * 128)
        for ft in range(16):
            bm = gp.get()
            for c in range(8):
                kb.op("pe", "matmul", [r_Wp[c], r_hT], [r_ps[bm]], inc=(c == 7), out=ps[bm],
                      lhsT=Wp[:, c, ft * 128:(ft + 1) * 128], rhs=hT[:, c, :], start=(c == 0), stop=(c == 7))
            k = qi % 2
            k3 = qi % 3
            qi += 1
            kb.op("act", "activation", [r_ps[bm]], [r_sq[k]], out=sq[k], in_=ps[bm], func=AF.Square)
            bn = gp.get()
            kb.op("pe", "matmul", [r_bones, r_sq[k]], [r_ps[bn]], out=ps[bn], lhsT=bones, rhs=sq[k], start=True,
                  stop=True)
            kb.op("act", "activation", [r_ps[bn]], [r_rn[k]], out=rn[k], in_=ps[bn], func=AF.Sqrt, scale=1.0 / 64,
                  bias=EPS)
            kb.op("dve", "reciprocal", [r_rn[k]], [r_rn[k]], out=rn[k], in_=rn[k])
            isk = 1 if ft >= 8 else 0
            kb.op("dve", "scalar_tensor_tensor", [r_rn[k], r_ps[bm], r_gq], [r_qo[k3]], out=qo[k3], in0=ps[bm],
                  scalar=gq[:, isk:isk + 1], in1=rn[k], op0=ALU.mult, op1=ALU.mult)
            pr = ft % 8
            if isk:
                kb.dma("pool", KT[pr * 128:(pr + 1) * 128, PAD + t0:PAD + t0 + 512], qo[k3], [r_qo[k3]], [])
            else:
                kb.dma("pool", QT[pr * 128:(pr + 1) * 128, t0:t0 + 512], qo[k3], [r_qo[k3]], [])
        for j in range(4):
            k = vi % 2
            vi += 1
            for hf in range(2):
                bm = gp.get()
                for c in range(8):
                    kb.op("pe", "matmul", [r_Wp[c], r_hT], [r_ps[bm]], inc=(c == 7), out=ps[bm],
                          lhsT=hT[:, c, j * 128:(j + 1) * 128], rhs=Wp[:, c, 2048 + hf * 512:2048 + (hf + 1) * 512],
                          start=(c == 0), stop=(c == 7))
                dst = v3(va[k], 16)[:, hf * 8:(hf + 1) * 8, 0:64]
                if hf == 0:
                    kb.op("act", "activation", [r_ps[bm]], [r_va[k]], out=dst, in_=v3(ps[bm], 8), func=AF.Copy)
                else:
                    kb.op("dve", "tensor_copy", [r_ps[bm]], [r_va[k]], out=dst, in_=v3(ps[bm], 8))
            r0 = PAD + t0 + j * 128
            kb.dma("pool", VD[r0:r0 + 128, :], va[k], [r_va[k]], [])


def phase_attn(kb, C, T, x_in, x_out, W, QT, KT, VD):
    A = phase_begin(kb, C)
    ps, psb, r_ps = C.ps, C.ps_bf, C.r_ps
    Wo = v3(A.bf(16 * 1024), 16)
    r_Wo = rr(16)
    alloc_wl(C, A)
    for h in range(16):
        load_weight(kb, C, Wo[0:64, h, :], r_Wo[h], W["attn_w_out"][0][h * 64:(h + 1) * 64, :], 1024, 1.0, parts=64)
    dist = {}
    r_dist = {}
    for nm, shp in ATT_SHAPES.items():
        dist[nm], r_dist[nm] = cload(kb, A, C.cd[nm], shp[1])
    NT = T // 512
    vfl, r_vfl = cload(kb, A, C.cd["vfl"], NT * NVT)
    ones = A.f32(64)
    r_ones = Res()
    kb.op("pool", "memset", [], [r_ones], ap=ones, constant=1.0)
    Kw = [A.bf(2560) for _ in range(4)]
    r_Kw = rr(4)
    Qs = [A.bf(512) for _ in range(4)]
    r_Qs = rr(4)
    Vt = [A.bf(520) for _ in range(NVT)]
    r_Vt = rr(NVT)
    sc = [A.f32(512) for _ in range(4)]
    r_sc = rr(4)
    pT = [A.bf(512) for _ in range(4)]
    r_pT = rr(4)
    oT = v3(A.bf(16 * 512), 16)
    r_oT = rr(16)
    rden = [A.f32(512) for _ in range(2)]
    r_rden = rr(2)
    bcs = [A.f32(512) for _ in range(2)]
    r_bcs = rr(2)
    xr = [A.f32(D) for _ in range(2)]
    r_xr = rr(2)
    xo = [A.f32(D) for _ in range(2)]
    r_xo = rr(2)
    gp = PsPool([0, 1, 2, 3])
    si = pi = oi = 0
    for t in range(NT):
        t0 = t * 512
        for hg in range(2):
            for pl in range(4):
                pr = hg * 4 + pl
                kb.dma("sp", Kw[pl], KT[pr * 128:(pr + 1) * 128, t0:t0 + 2560], [], [r_Kw[pl]])
                kb.dma("sp", Qs[pl], QT[pr * 128:(pr + 1) * 128, t0:t0 + 512], [], [r_Qs[pl]])
            vspec = []
            for m in range(5):
                vspec.append((t0 + 960 + 128 * m, 1, 128))
            for r in range(4):
                for kbk in range(2):
                    vspec.append((t0 + 768 + r + 512 * kbk, 4, 128))
            for r in range(16):
                for kbk in range(2):
                    vspec.append((t0 + r + 2048 * kbk, 16, 128 if kbk == 0 else 32))
            for vi_, (row0, stp, n) in enumerate(vspec):
                src = VD[row0:row0 + stp * (n - 1) + 1:stp, hg * 520:(hg + 1) * 520]
                kb.dma("sp" if vi_ % 2 == 0 else "act", Vt[vi_][0:n, :], src, [], [r_Vt[vi_]])
                col = t * NVT + vi_
                if vi_ % 2 == 0:
                    kb.op("dve", "tensor_scalar", [r_Vt[vi_], r_vfl], [r_Vt[vi_]], out=Vt[vi_][0:n, :],
                          in0=Vt[vi_][0:n, :], scalar1=vfl[0:n, col:col + 1], scalar2=None, op0=ALU.mult)
                else:
                    kb.op("act", "activation", [r_Vt[vi_], r_vfl], [r_Vt[vi_]], out=Vt[vi_][0:n, :],
                          in_=Vt[vi_][0:n, :], func=AF.Copy, scale=vfl[0:n, col:col + 1])
            for hq in range(0, 8, 4):
                hls = [hq + u for u in range(4)]
                first = {hl: True for hl in hls}
                for name, dil, nq, nblk in (("d1", 1, 128, 4), ("d4", 4, 128, 4), ("d16", 16, 32, 16)):
                    for kbk in range(2):
                        for hl in hls:
                            h = hg * 8 + hl
                            pl, e = hl // 2, hl % 2
                            hp = slice(64 * e, 64 * e + 64)
                            po = 4 + hl % 4
                            nk = 32 if (dil == 16 and kbk == 1) else 128
                            bs = gp.get()
                            for b in range(nblk):
                                if dil == 1:
                                    k0 = 960 + 128 * (b + kbk)
                                    kcols = Kw[pl][hp, k0:k0 + nk]
                                    qcols = Qs[pl][hp, b * 128:(b + 1) * 128]
                                elif dil == 4:
                                    k0 = 768 + b + 512 * kbk
                                    kcols = Kw[pl][hp, k0:k0 + 4 * (nk - 1) + 1:4]
                                    qcols = Qs[pl][hp, b:b + 4 * 127 + 1:4]
                                else:
                                    k0 = b + 2048 * kbk
                                    kcols = Kw[pl][hp, k0:k0 + 16 * (nk - 1) + 1:16]
                                    qcols = Qs[pl][hp, b:b + 16 * 31 + 1:16]
                                kb.op("pe", "matmul", [r_Kw[pl], r_Qs[pl]], [r_ps[bs]], inc=(b == nblk - 1),
                                      out=ps[bs][0:nk, b * nq:(b + 1) * nq], lhsT=kcols, rhs=qcols, start=True,
                                      stop=True)
                            k2 = si % 4
                            si += 1
                            dn = f"{name}k{kbk}"
                            kb.op("dve", "scalar_tensor_tensor", [r_ps[bs], r_dist[dn]], [r_sc[k2]],
                                  out=v3(sc[k2][0:nk, :], nblk),
                                  in0=dist[dn][0:nk, :].unsqueeze(1).to_broadcast([nk, nblk, nq]), scalar=SLOPES[h],
                                  in1=v3(ps[bs][0:nk, :], nblk), op0=ALU.mult, op1=ALU.add)
                            k3 = pi % 4
                            pi += 1
                            kb.op("act", "activation", [r_sc[k2]], [r_pT[k3]], out=pT[k3][0:nk, :],
                                  in_=sc[k2][0:nk, :], func=AF.Exp)
                            for b in range(nblk):
                                if dil == 1:
                                    vidx = b + kbk
                                    ocols = ps[po][0:65, b * 128:(b + 1) * 128]
                                elif dil == 4:
                                    vidx = 5 + b * 2 + kbk
                                    ocols = ps[po][0:65, b:b + 4 * 127 + 1:4]
                                else:
                                    vidx = 13 + b * 2 + kbk
                                    ocols = ps[po][0:65, b:b + 16 * 31 + 1:16]
                                last = (dil == 16 and kbk == 1 and b == nblk - 1)
                                kb.op("pe", "matmul", [r_Vt[vidx], r_pT[k3]], [r_ps[po]], inc=(b == nblk - 1),
                                      out=ocols, lhsT=Vt[vidx][0:nk, hl * 65:(hl + 1) * 65],
                                      rhs=pT[k3][0:nk, b * nq:(b + 1) * nq], start=first[hl], stop=last)
                                first[hl] = False
                for hl in hls:
                    h = hg * 8 + hl
                    po = 4 + hl % 4
                    kd = hl % 2
                    kb.op("dve", "reciprocal", [r_ps[po]], [r_rden[kd]], out=rden[kd][64:65, :], in_=ps[po][64:65, :])
                    bb = gp.get()
                    kb.op("pe", "matmul", [r_ones, r_rden[kd]], [r_ps[bb]], out=ps[bb][0:64, :],
                          lhsT=ones[64:65, 0:64], rhs=rden[kd][64:65, :], start=True, stop=True)
                    kb.op("act", "activation", [r_ps[bb]], [r_bcs[kd]], out=bcs[kd][0:64, :], in_=ps[bb][0:64, :],
                          func=AF.Copy)
                    kb.op("dve", "tensor_tensor", [r_ps[po], r_bcs[kd]], [r_oT[h]], out=oT[0:64, h, :],
                          in0=ps[po][0:64, :], in1=bcs[kd][0:64, :], op=ALU.mult)
        for j in range(4):
            r0 = t0 + j * 128
            k = oi % 2
            oi += 1
            kb.dma("sp", xr[k], x_in[r0:r0 + 128, :], [], [r_xr[k]])
            for hf in range(2):
                bo = gp.get()
                for h in range(16):
                    kb.op("pe", "matmul", [r_oT[h], r_Wo[h]], [r_ps[bo]], inc=(h == 15), out=ps[bo],
                          lhsT=oT[0:64, h, j * 128:(j + 1) * 128], rhs=Wo[0:64, h, hf * 512:(hf + 1) * 512],
                          start=(h == 0), stop=(h == 15))
                kb.op("dve", "tensor_tensor", [r_ps[bo], r_xr[k]], [r_xo[k]], out=xo[k][:, hf * 512:(hf + 1) * 512],
                      in0=ps[bo], in1=xr[k][:, hf * 512:(hf + 1) * 512], op=ALU.add)
            kb.dma("pool", x_out[r0:r0 + 128, :], xo[k], [r_xo[k]], [])
```
